# Optimizing a Trainium2 kernel written in Bass

```python
import math
import numpy as np
import jax
import jax.numpy as jnp
from jax import lax

D_MODEL = 1024
BATCH = 8
SEQ = 2048
DEPTH = 4

D_MIX = D_MODEL
HEAD_DIM = 64
FOX_WIDTH = D_MIX // 4
DIL_WIDTH = D_MIX // 4
HGRN_WIDTH = D_MIX // 2
FOX_HEADS = FOX_WIDTH // HEAD_DIM
DIL_HEADS = DIL_WIDTH // HEAD_DIM
HGRN_EXPAND = 128
HGRN_HEADS = HGRN_WIDTH // HGRN_EXPAND
HGRN_VDIM = HGRN_WIDTH // HGRN_HEADS
HGRN_FDIM = HGRN_HEADS * HGRN_EXPAND
HGRN_CHUNK = 64
Q_BLOCK = 128
DILATED_PATTERNS = ((128, 1), (512, 4), (2048, 16))
ROPE_DIM = HEAD_DIM // 4
ROPE_THETA = 500000.0
D_FF = ((8 * D_MODEL // 3 + 127) // 128) * 128
EPS = 1e-6
NEG_BIG = -1e30
LB_FLOOR = 1e-30

SPLIT_SIZES = (FOX_WIDTH, FOX_WIDTH, FOX_WIDTH, FOX_HEADS,
               DIL_WIDTH, DIL_WIDTH, DIL_WIDTH,
               HGRN_FDIM, HGRN_FDIM, HGRN_WIDTH, HGRN_WIDTH)
N_IN = sum(SPLIT_SIZES)
SPLIT_POINTS = tuple(int(s) for s in np.cumsum(SPLIT_SIZES)[:-1])

kernel_name = "hybrid_fox_dilated_hgrn2_macaron"


def _rmsnorm(x, g):
    xf = x.astype(jnp.float32)
    y = xf * lax.rsqrt(jnp.mean(xf * xf, axis=-1, keepdims=True) + EPS)
    return (y * g.astype(jnp.float32)).astype(x.dtype)


def _swiglu(h, wg, wu, wd):
    return (jax.nn.silu(h @ wg) * (h @ wu)) @ wd


def _partial_rope(t, positions):
    half = ROPE_DIM // 2
    freqs = ROPE_THETA ** (-jnp.arange(0, ROPE_DIM, 2, dtype=jnp.float32) / ROPE_DIM)
    ang = positions.astype(jnp.float32)[:, :, None] * freqs
    cos = jnp.cos(ang)[:, :, None, :]
    sin = jnp.sin(ang)[:, :, None, :]
    tf = t.astype(jnp.float32)
    x1, x2, rest = tf[..., :half], tf[..., half:ROPE_DIM], tf[..., ROPE_DIM:]
    out = jnp.concatenate([x1 * cos - x2 * sin, x2 * cos + x1 * sin, rest], axis=-1)
    return out.astype(t.dtype)


def _fox_attention(q, k, v, c):
    B, H, T, dh = q.shape
    nb = T // Q_BLOCK
    scale = dh ** -0.5
    qb = q.reshape(B, H, nb, Q_BLOCK, dh).transpose(2, 0, 1, 3, 4)
    cb = c.reshape(B, H, nb, Q_BLOCK).transpose(2, 0, 1, 3)
    kpos = jnp.arange(T)

    def block(args):
        qi, ci, n = args
        s = jnp.einsum('bhqd,bhkd->bhqk', qi, k).astype(jnp.float32) * scale
        s = s + ci[..., None] - c[:, :, None, :]
        qpos = n * Q_BLOCK + jnp.arange(Q_BLOCK)
        mask = kpos[None, :] <= qpos[:, None]
        p = jax.nn.softmax(jnp.where(mask, s, NEG_BIG), axis=-1)
        return jnp.einsum('bhqk,bhkd->bhqd', p.astype(v.dtype), v)

    out = lax.map(block, (qb, cb, jnp.arange(nb)))
    return out.transpose(1, 2, 0, 3, 4).reshape(B, H, T, dh)


def _dilated_branch(q, k, v, window, dilation):
    B, H, T, dh = q.shape
    L = T // dilation
    w = window // dilation
    blk = min(w, L)
    nb = -(-L // blk)
    Lp = nb * blk
    scale = dh ** -0.5

    def to_blocks(t):
        t = t.reshape(B, H, L, dilation, dh).transpose(0, 1, 3, 2, 4)
        t = jnp.pad(t, ((0, 0), (0, 0), (0, 0), (0, Lp - L), (0, 0)))
        return t.reshape(B, H, dilation, nb, blk, dh)

    def with_prev(t):
        prev = jnp.pad(t, ((0, 0), (0, 0), (0, 0), (1, 0), (0, 0), (0, 0)))[:, :, :, :nb]
        return jnp.concatenate([prev, t], axis=4)

    qb = to_blocks(q)
    kc = with_prev(to_blocks(k))
    vc = with_prev(to_blocks(v))
    s = jnp.einsum('bhrnqd,bhrnkd->bhrnqk', qb, kc).astype(jnp.float32) * scale
    i = jnp.arange(blk)[:, None]
    j = jnp.arange(2 * blk)[None, :]
    rel = blk + i - j
    first = (jnp.arange(nb)[:, None, None] == 0) & (j < blk)[None]
    mask = ((rel >= 0) & (rel <= w))[None] & ~first
    s = jnp.where(mask, s, NEG_BIG)
    m = jnp.max(s, axis=-1, keepdims=True)
    p = jnp.exp(s - m)
    den = jnp.sum(p, axis=-1)
    o = jnp.einsum('bhrnqk,bhrnkd->bhrnqd', p, vc.astype(jnp.float32)) / den[..., None]
    lse = m[..., 0] + jnp.log(den)
    o = o.reshape(B, H, dilation, Lp, dh)[:, :, :, :L].transpose(0, 1, 3, 2, 4).reshape(B, H, T, dh)
    lse = lse.reshape(B, H, dilation, Lp)[..., :L].transpose(0, 1, 3, 2).reshape(B, H, T)
    return o, lse


def _dilated_attention(q, k, v):
    outs, lses = [], []
    for window, dilation in DILATED_PATTERNS:
        o, lse = _dilated_branch(q, k, v, window, dilation)
        outs.append(o)
        lses.append(lse)
    wts = jax.nn.softmax(jnp.stack(lses, 0), axis=0)
    return jnp.sum(wts[..., None] * jnp.stack(outs, 0), axis=0)


def _hgrn2(q_raw, f_raw, i_raw, g_raw, lb, norm_w):
    B, T, _ = q_raw.shape
    H, E, V, C = HGRN_HEADS, HGRN_EXPAND, HGRN_VDIM, HGRN_CHUNK
    nc = T // C
    lbf = jnp.clip(lb.astype(jnp.float32), 0.0, 1.0 - 1e-6)
    z = f_raw.astype(jnp.float32)
    log_f = jnp.logaddexp(jnp.log(jnp.maximum(lbf, LB_FLOOR)),
                          jnp.log1p(-lbf) + jax.nn.log_sigmoid(z))
    kk = (1.0 - lbf) * jax.nn.sigmoid(-z)
    qq = jax.nn.silu(q_raw.astype(jnp.float32))
    vv = i_raw.astype(jnp.float32)

    def chunks(t, d):
        return t.reshape(B, nc, C, H, d).transpose(1, 0, 3, 2, 4)

    causal = jnp.arange(C)[:, None] >= jnp.arange(C)[None, :]

    def step(S, inp):
        q, k, v, lf = inp
        b = jnp.cumsum(lf, axis=2)
        inter = jnp.einsum('bhte,bhev->bhtv', q * jnp.exp(b), S)
        diff = b[:, :, :, None, :] - b[:, :, None, :, :]
        D = jnp.exp(jnp.where(causal[None, None, :, :, None], diff, NEG_BIG))
        A = jnp.einsum('bhtse,bhse->bhts', q[:, :, :, None, :] * D, k)
        intra = jnp.einsum('bhts,bhsv->bhtv', A, v)
        b_last = b[:, :, -1, :]
        S_new = jnp.exp(b_last)[..., None] * S + jnp.einsum(
            'bhse,bhsv->bhev', k * jnp.exp(b_last[:, :, None, :] - b), v)
        return S_new, inter + intra

    S0 = jnp.zeros((B, H, E, V), jnp.float32)
    _, o = lax.scan(step, S0, (chunks(qq, E), chunks(kk, E), chunks(vv, V), chunks(log_f, E)))
    o = o.transpose(1, 0, 3, 2, 4).reshape(B, T, H, V)
    o = o * lax.rsqrt(jnp.mean(o * o, axis=-1, keepdims=True) + EPS)
    o = o.reshape(B, T, H * V) * norm_w.astype(jnp.float32)
    return o * jax.nn.sigmoid(g_raw.astype(jnp.float32))


def _mixing(h, positions, w_in, w_out, fox_b, lb, hgrn_norm):
    B, T, _ = h.shape
    proj = h @ w_in
    (fq, fk, fv, ff, dq, dk, dv, hq, hf, hi, hg) = jnp.split(proj, SPLIT_POINTS, axis=-1)

    def heads(t, n):
        return t.reshape(B, T, n, HEAD_DIM)

    log_fg = jax.nn.log_sigmoid((ff + fox_b).astype(jnp.float32))
    c = jnp.cumsum(log_fg, axis=1).transpose(0, 2, 1)
    tr = lambda t: t.transpose(0, 2, 1, 3)
    oa = _fox_attention(tr(heads(fq, FOX_HEADS)), tr(heads(fk, FOX_HEADS)),
                        tr(heads(fv, FOX_HEADS)), c)
    oa = oa.transpose(0, 2, 1, 3).reshape(B, T, FOX_WIDTH)

    qd = _partial_rope(heads(dq, DIL_HEADS), positions)
    kd = _partial_rope(heads(dk, DIL_HEADS), positions)
    ob = _dilated_attention(tr(qd), tr(kd), tr(heads(dv, DIL_HEADS)))
    ob = ob.transpose(0, 2, 1, 3).reshape(B, T, DIL_WIDTH)

    oc = _hgrn2(hq, hf, hi, hg, lb, hgrn_norm)

    o = jnp.concatenate([oa.astype(h.dtype), ob.astype(h.dtype), oc.astype(h.dtype)], axis=-1)
    return o @ w_out


def setup_inputs(seed: int = 0) -> dict:
    key = jax.random.key(seed)
    ks = jax.random.split(key, 20)
    f32 = jnp.float32
    nrm = lambda k, shape, fan: jax.random.normal(k, shape, f32) * fan ** -0.5
    gain = lambda k, shape: 1.0 + 0.02 * jax.random.normal(k, shape, f32)
    x = jax.random.normal(ks[0], (BATCH, SEQ, D_MODEL), f32)
    positions = jnp.broadcast_to(jnp.arange(SEQ, dtype=jnp.int32)[None, :], (BATCH, SEQ))
    return {
        "x": x,
        "positions": positions,
        "ffn1_norm": gain(ks[1], (DEPTH, D_MODEL)),
        "ffn1_w_gate": nrm(ks[2], (DEPTH, D_MODEL, D_FF), D_MODEL),
        "ffn1_w_up": nrm(ks[3], (DEPTH, D_MODEL, D_FF), D_MODEL),
        "ffn1_w_down": nrm(ks[4], (DEPTH, D_FF, D_MODEL), D_FF),
        "mix_norm": gain(ks[5], (DEPTH, D_MODEL)),
        "w_in": nrm(ks[6], (DEPTH, D_MODEL, N_IN), D_MODEL),
        "fox_forget_bias": 2.0 + 0.5 * jax.random.normal(ks[7], (DEPTH, FOX_HEADS), f32),
        "hgrn_lower_bounds": 0.1 * jax.random.normal(ks[8], (DEPTH, HGRN_FDIM), f32),
        "hgrn_out_norm": gain(ks[9], (DEPTH, HGRN_WIDTH)),
        "w_out": nrm(ks[10], (DEPTH, D_MIX, D_MODEL), D_MIX),
        "ffn2_norm": gain(ks[11], (DEPTH, D_MODEL)),
        "ffn2_w_gate": nrm(ks[12], (DEPTH, D_MODEL, D_FF), D_MODEL),
        "ffn2_w_up": nrm(ks[13], (DEPTH, D_MODEL, D_FF), D_MODEL),
        "ffn2_w_down": nrm(ks[14], (DEPTH, D_FF, D_MODEL), D_FF),
        "final_norm": gain(ks[15], (D_MODEL,)),
    }


def reference(x, positions, ffn1_norm, ffn1_w_gate, ffn1_w_up, ffn1_w_down, mix_norm, w_in,
              fox_forget_bias, hgrn_lower_bounds, hgrn_out_norm, w_out, ffn2_norm,
              ffn2_w_gate, ffn2_w_up, ffn2_w_down, final_norm):
    sm = jax.nn.softmax(hgrn_lower_bounds.astype(jnp.float32), axis=0)
    lbs = jnp.cumsum(sm, axis=0) - sm[0:1]
    for i in range(DEPTH):
        h = _rmsnorm(x, ffn1_norm[i])
        x = x + 0.5 * _swiglu(h, ffn1_w_gate[i], ffn1_w_up[i], ffn1_w_down[i])
        h = _rmsnorm(x, mix_norm[i])
        x = x + _mixing(h, positions, w_in[i], w_out[i], fox_forget_bias[i], lbs[i],
                        hgrn_out_norm[i])
        h = _rmsnorm(x, ffn2_norm[i])
        x = x + 0.5 * _swiglu(h, ffn2_w_gate[i], ffn2_w_up[i], ffn2_w_down[i])
    return _rmsnorm(x, final_norm)
```

```python
import math
import types
from contextlib import ExitStack

import numpy as np
import concourse.bass as bass
import concourse.mybir as mybir
from concourse.bass_utils import run_bass_kernel_spmd

F32 = mybir.dt.float32
BF16 = mybir.dt.bfloat16
I32 = mybir.dt.int32
AF = mybir.ActivationFunctionType
ALU = mybir.AluOpType
AX = mybir.AxisListType

D = 1024
DFF = 2816
NIN = 3588
EPS = 1e-6
ENGS = ("pe", "act", "dve", "pool", "sp")


def _freeze(fn):
    if fn is None or fn.__closure__ is None:
        return fn
    cells = []
    for c in fn.__closure__:
        try:
            cells.append(types.CellType(c.cell_contents))
        except ValueError:
            cells.append(c)
    return types.FunctionType(fn.__code__, fn.__globals__, fn.__name__, fn.__defaults__, tuple(cells))


class Prog:
    def __init__(self):
        self.ops = {e: [] for e in ENGS}
        self.last_w = {}
        self.readers = {}
        self.chans = []

    def op(self, eng, fn, reads=(), writes=(), signal=True, chan=None):
        deps = set()
        for k in reads:
            w = self.last_w.get(k)
            if w is not None:
                deps.add(w)
        for k in writes:
            w = self.last_w.get(k)
            if w is not None:
                deps.add(w)
            deps |= self.readers.get(k, set())
        idx = len(self.ops[eng])
        if chan is not None and chan not in self.chans:
            self.chans.append(chan)
        self.ops[eng].append(dict(fn=_freeze(fn), deps=deps, signal=signal, chan=chan))
        ev = (eng, idx)
        for k in reads:
            self.readers.setdefault(k, set()).add(ev)
        for k in writes:
            self.last_w[k] = ev
            self.readers[k] = set()
        return ev

    _uid = [0]

    def emit(self, nc, stack):
        sems = {}
        Prog._uid[0] += 1
        u = "_%d" % Prog._uid[0]
        stack.enter_context(nc.cleanup_on_exit())
        for e in ENGS:
            sems[e] = nc.alloc_semaphore(name="s_" + e + u)
        for c in self.chans:
            sems[("ch", c)] = nc.alloc_semaphore(name="c_" + str(c) + u)
        for e in ENGS:
            cnt = 0
            for o in self.ops[e]:
                if o["chan"] is None and o["signal"]:
                    cnt += 1
                    o["val"] = cnt
            nxt = None
            for o in reversed(self.ops[e]):
                if o["chan"] is not None:
                    continue
                if o["signal"]:
                    nxt = o["val"]
                else:
                    o["val"] = nxt
        ccnt = {}
        for e in ENGS:
            for o in self.ops[e]:
                if o["chan"] is not None:
                    ccnt[o["chan"]] = ccnt.get(o["chan"], 0) + 16
                    o["val"] = ccnt[o["chan"]]

        def resolve(ev):
            e, i = ev
            o = self.ops[e][i]
            if o["chan"] is not None:
                return sems[("ch", o["chan"])], o["val"]
            assert o["val"] is not None, ("dependency on unsignaled op", ev)
            return sems[e], o["val"]

        def emit_eng(ename, eng):
            waited = {}
            for o in self.ops[ename]:
                for d in sorted(o["deps"]):
                    if d[0] == "pe" and ename == "pe":
                        continue
                    s, v = resolve(d)
                    key = id(s)
                    if waited.get(key, 0) < v:
                        eng.wait_ge(s, v)
                        waited[key] = v
                if o["fn"] is None:
                    continue
                ins = o["fn"](eng)
                if o["chan"] is not None:
                    ins.then_inc(sems[("ch", o["chan"])], 16)
                elif o["signal"]:
                    ins.then_inc(sems[ename], 1)
            if ename == "sp":
                for c, v in ccnt.items():
                    eng.wait_ge(sems[("ch", c)], v)

        with nc.Block(no_gpsimd_drain=True) as blk:
            @blk.tensor
            def _(e):
                emit_eng("pe", e)

            @blk.scalar
            def _(e):
                emit_eng("act", e)

            @blk.vector
            def _(e):
                emit_eng("dve", e)

            @blk.gpsimd
            def _(e):
                emit_eng("pool", e)

            @blk.sync
            def _(e):
                emit_eng("sp", e)


def make_consts():
    c = np.zeros((128, 5 * 128 + 16), np.float32)
    mt = np.zeros((128, 2048), np.float32)
    c[:, 0:128] = np.eye(128, dtype=np.float32)
    c[:, 128:256] = np.triu(np.ones((128, 128), np.float32))
    c[63, 256:384] = 1.0
    c[:, 384:512] = 1.0
    c[31, 512:576] = 1.0
    i = np.arange(128)[:, None]
    j = np.arange(128)[None, :]
    for sl in range(16):
        db = 15 - sl
        d = db * 128 + j - i
        m = ((d >= 0) & (d <= 128)).astype(np.float32)
        m += ((d >= 0) & (d % 4 == 0) & (d <= 512)).astype(np.float32)
        m += ((d >= 0) & (d % 16 == 0) & (d <= 2048)).astype(np.float32)
        mt[:, sl * 128:(sl + 1) * 128] = m
    fr = 500000.0 ** (-np.arange(0, 16, 2, dtype=np.float32) / 16)
    c[:, 640:648] = fr.astype(np.float32)[None, :]
    return c, mt


NCONST = 5 * 128 + 16


class Builder:
    def __init__(self, T, depth, stages=("ffn1", "mix", "ffn2")):
        self.T = T
        self.TT = T // 128
        self.depth = depth
        self.stages = stages
        self.TB = min(512, T)
        self.NTB = T // self.TB
        self.NCH = T // 64
        nc = bass.Bass("TRN2", target_bir_lowering=False)
        self.nc = nc
        dt = lambda name, shape, d=F32, kind="ExternalInput": nc.dram_tensor(name, shape, d, kind=kind).ap()
        L = depth
        self.d = dict(
            x=dt("x", [T, D]), positions=dt("positions", [T], I32),
            ffn1_norm=dt("ffn1_norm", [L, D]), ffn1_w_gate=dt("ffn1_w_gate", [L, D, DFF]),
            ffn1_w_up=dt("ffn1_w_up", [L, D, DFF]), ffn1_w_down=dt("ffn1_w_down", [L, DFF, D]),
            mix_norm=dt("mix_norm", [L, D]), w_in=dt("w_in", [L, D, NIN]),
            fox_forget_bias=dt("fox_forget_bias", [L * 4]), hgrn_lower_bounds=dt("hgrn_lower_bounds", [L * 4, 128]),
            hgrn_out_norm=dt("hgrn_out_norm", [L * 4, 128]), w_out=dt("w_out", [L, D, D]),
            ffn2_norm=dt("ffn2_norm", [L, D]), ffn2_w_gate=dt("ffn2_w_gate", [L, D, DFF]),
            ffn2_w_up=dt("ffn2_w_up", [L, D, DFF]), ffn2_w_down=dt("ffn2_w_down", [L, DFF, D]),
            final_norm=dt("final_norm", [D]), consts=dt("consts", [128, NCONST]), mtab=dt("mtab", [128, 2048]),
        )
        self.out = dt("out", [T, D], F32, "ExternalOutput")

    def scope(self):
        return ExitStack()

    def sb(self, st, name, shape, dtype):
        self.uid = getattr(self, "uid", 0) + 1
        return st.enter_context(self.nc.sbuf_tensor("%s_%d" % (name, self.uid), shape, dtype))

    def build(self):
        nc = self.nc
        T, TT = self.T, self.TT
        with ExitStack() as st:
            self.x = self.sb(st, "xres", [128, TT, D], F32)
            self.hT = self.sb(st, "hT", [128, 8, T], BF16)
            self.cst = self.sb(st, "cst", [128, NCONST], F32)
            self.identb = self.sb(st, "identb", [128, 128], BF16)
            self.trib = self.sb(st, "trib", [128, 128], BF16)
            self.onesb = self.sb(st, "onesb", [128, 128], BF16)
            self.mtb = self.sb(st, "mtb", [128, 16 * 128], BF16)
            self.cosb = self.sb(st, "cosb", [128, TT, 8], F32)
            self.sinb = self.sb(st, "sinb", [128, TT, 8], F32)
            self.lb = self.sb(st, "lb", [128, self.depth * 4], F32)
            self.oml = self.sb(st, "oml", [128, self.depth * 4], F32)
            self.nw = self.sb(st, "nw", [128, self.depth * 4], F32)
            self.fbias = self.sb(st, "fbias", [128, self.depth * 4], F32)
            self.epsb = self.sb(st, "epsb", [128, 1], F32)
            self.ps = [st.enter_context(nc.psum_tensor("ps%d" % i, [128, 512], F32)) for i in range(8)]
            self.identf = self.cst[:, 0:128]
            self.trif = self.cst[:, 128:256]
            self.e63f = self.cst[:, 256:384]
            self.onesf = self.cst[:, 384:512]
            self.phase_init()
            for l in range(self.depth):
                if "ffn1" in self.stages:
                    self.phase_ffn(l, "ffn1")
                if "mix" in self.stages:
                    with ExitStack() as ms:
                        self.oT = self.sb(ms, "oT", [128, 8, T], BF16)
                        self.phase_attn(l, "fox")
                        self.phase_attn(l, "dil")
                        self.phase_hgrn(l)
                        if "ffn2" in self.stages:
                            self.phase_ffn(l, "ffn2", with_wout=True)
                        else:
                            self.phase_wout(l)
                elif "ffn2" in self.stages:
                    self.phase_ffn(l, "ffn2")
            self.phase_norm(self.d["final_norm"], "nf", final=True)
        return nc

    def phase_init(self):
        nc, T, TT = self.nc, self.T, self.TT
        L4 = self.depth * 4
        with ExitStack() as st:
            P = Prog()
            posi = self.sb(st, "posi", [128, TT], I32)
            posf = self.sb(st, "posf", [128, TT], F32)
            ang = self.sb(st, "ang", [128, TT, 8], F32)
            tmp = self.sb(st, "tmpi", [128, TT, 8], F32)
            raw = self.sb(st, "rawlb", [L4, 256], F32)
            tl = self.sb(st, "tl", [128, 2 * L4], F32)
            mx = self.sb(st, "mx", [128, 4], F32)
            sm = self.sb(st, "sm", [128, 4], F32)
            x, cst = self.x, self.cst
            P.op("sp", lambda e: e.dma_start(out=cst[:], in_=self.d["consts"]), writes=["cst"], chan="cst")
            nx = 4 if TT >= 4 else TT
            per = TT // nx
            xin = self.d["x"].rearrange("(tt p) d -> p tt d", p=128)
            for i in range(nx):
                P.op("sp", lambda e, i=i: e.dma_start(out=x[:, i * per:(i + 1) * per, :], in_=xin[:, i * per:(i + 1) * per, :]),
                     writes=[("x", t) for t in range(i * per, (i + 1) * per)], chan="x%d" % i)
            P.op("sp", lambda e: e.dma_start(out=posi[:], in_=self.d["positions"].rearrange("(tt p) -> p tt", p=128),
                                             allow_slow_non_contiguous=True), writes=["posi"], chan="pos")
            P.op("sp", lambda e: e.dma_start(out=raw[:, 0:128], in_=self.d["hgrn_lower_bounds"]), writes=["raw0"], chan="r0")
            P.op("sp", lambda e: e.dma_start(out=raw[:, 128:256], in_=self.d["hgrn_out_norm"]), writes=["raw1"], chan="r1")
            P.op("sp", lambda e: e.dma_start(out=self.fbias[:], in_=self.d["fox_forget_bias"].partition_broadcast(128)),
                 writes=["fbias"], chan="fb")
            P.op("dve", lambda e: e.memset(self.epsb[:], EPS), writes=["epsb"])
            P.op("dve", lambda e: e.tensor_copy(out=self.identb[:], in_=cst[:, 0:128]), reads=["cst"], writes=["identb"])
            P.op("dve", lambda e: e.tensor_copy(out=self.trib[:], in_=cst[:, 128:256]), reads=["cst"], writes=["trib"])
            P.op("dve", lambda e: e.tensor_copy(out=self.onesb[:], in_=cst[:, 384:512]), reads=["cst"], writes=["onesb"])
            P.op("pool", lambda e: e.dma_start(out=self.mtb[:], in_=self.d["mtab"]), writes=["mtb"], chan="mtb")
            P.op("dve", lambda e: e.tensor_copy(out=posf[:], in_=posi[:]), reads=["posi"], writes=["posf"])
            fr = cst[:, 640:648]
            P.op("dve", lambda e: e.tensor_tensor(out=ang[:], in0=posf[:].unsqueeze(2).broadcast_to([128, TT, 8]),
                                                  in1=fr.unsqueeze(1).broadcast_to([128, TT, 8]), op=ALU.mult),
                 reads=["posf", "cst"], writes=["ang"])
            twopi = 2.0 * math.pi
            C1 = 6.28125
            C2 = twopi - C1
            ki = self.sb(st, "ki", [128, TT, 8], I32)
            kf = self.sb(st, "kf", [128, TT, 8], F32)
            a2 = self.sb(st, "a2", [128, TT, 8], F32)
            for nm, dst, sh in (("sin", self.sinb, 0.0), ("cos", self.cosb, 0.5 * math.pi)):
                P.op("dve", lambda e: e.tensor_scalar(out=a2[:], in0=ang[:], scalar1=sh, scalar2=None, op0=ALU.add), reads=["ang"], writes=["a2"])
                P.op("dve", lambda e: e.tensor_scalar(out=tmp[:], in0=a2[:], scalar1=1.0 / twopi, scalar2=None, op0=ALU.mult), reads=["a2"], writes=["tmpi"])
                P.op("dve", lambda e: e.tensor_copy(out=ki[:], in_=tmp[:]), reads=["tmpi"], writes=["ki"])
                P.op("dve", lambda e: e.tensor_copy(out=kf[:], in_=ki[:]), reads=["ki"], writes=["kf"])
                P.op("dve", lambda e: e.scalar_tensor_tensor(out=tmp[:], in0=kf[:], scalar=-C1, in1=a2[:], op0=ALU.mult, op1=ALU.add),
                     reads=["kf", "a2"], writes=["tmpi"])
                P.op("dve", lambda e: e.scalar_tensor_tensor(out=tmp[:], in0=kf[:], scalar=-C2, in1=tmp[:], op0=ALU.mult, op1=ALU.add),
                     reads=["kf", "tmpi"], writes=["tmpi"])
                P.op("dve", lambda e: e.tensor_scalar(out=kf[:], in0=tmp[:], scalar1=math.pi, scalar2=twopi, op0=ALU.is_gt, op1=ALU.mult),
                     reads=["tmpi"], writes=["kf"])
                P.op("dve", lambda e: e.tensor_tensor(out=tmp[:], in0=tmp[:], in1=kf[:], op=ALU.subtract), reads=["tmpi", "kf"], writes=["tmpi"])
                P.op("dve", lambda e: e.tensor_scalar(out=kf[:], in0=tmp[:], scalar1=-math.pi, scalar2=twopi, op0=ALU.is_lt, op1=ALU.mult),
                     reads=["tmpi"], writes=["kf"])
                P.op("dve", lambda e: e.tensor_tensor(out=tmp[:], in0=tmp[:], in1=kf[:], op=ALU.add), reads=["tmpi", "kf"], writes=["tmpi"])
                P.op("act", lambda e: e.activation(out=dst[:], in_=tmp[:], func=AF.Sin), reads=["tmpi"], writes=[nm])
            pst = self.ps[0]
            P.op("pe", lambda e: e.matmul(pst[:, 0:L4], lhsT=raw[:, 0:128], rhs=cst[0:L4, 0:L4], start=True, stop=True),
                 reads=["raw0", "cst"], writes=["ps0"])
            P.op("pe", lambda e: e.matmul(pst[:, L4:2 * L4], lhsT=raw[:, 128:256], rhs=cst[0:L4, 0:L4], start=True, stop=True),
                 reads=["raw1", "cst"], writes=["ps0"])
            P.op("dve", lambda e: e.tensor_copy(out=tl[:], in_=pst[:, 0:2 * L4]), reads=["ps0"], writes=["tl"])
            P.op("dve", lambda e: e.tensor_copy(out=self.nw[:], in_=tl[:, L4:2 * L4]), reads=["tl"], writes=["nw"])
            lbv = tl[:, 0:L4].rearrange("p (l h) -> p h l", h=4)
            P.op("dve", lambda e: e.tensor_reduce(out=mx[:], in_=lbv, axis=AX.X, op=ALU.max), reads=["tl"], writes=["mx"])
            P.op("dve", lambda e: e.tensor_tensor(out=lbv, in0=lbv, in1=mx[:].unsqueeze(2).broadcast_to([128, 4, self.depth]), op=ALU.subtract),
                 reads=["tl", "mx"], writes=["tl"])
            P.op("act", lambda e: e.activation(out=tl[:, 0:L4], in_=tl[:, 0:L4], func=AF.Exp), reads=["tl"], writes=["tl"])
            P.op("dve", lambda e: e.tensor_reduce(out=sm[:], in_=lbv, axis=AX.X, op=ALU.add), reads=["tl"], writes=["sm"])
            P.op("dve", lambda e: e.reciprocal(out=sm[:], in_=sm[:]), reads=["sm"], writes=["sm"])
            P.op("dve", lambda e: e.tensor_tensor(out=lbv, in0=lbv, in1=sm[:].unsqueeze(2).broadcast_to([128, 4, self.depth]), op=ALU.mult),
                 reads=["tl", "sm"], writes=["tl"])
            P.op("dve", lambda e: e.memset(self.lb[:, 0:4], 0.0), writes=["lb"])
            for l in range(1, self.depth):
                P.op("dve", lambda e, l=l: e.tensor_tensor(out=self.lb[:, l * 4:(l + 1) * 4], in0=self.lb[:, (l - 1) * 4:l * 4],
                                                           in1=tl[:, l * 4:(l + 1) * 4], op=ALU.add), reads=["lb", "tl"], writes=["lb"])
            P.op("dve", lambda e: e.tensor_scalar(out=self.lb[:], in0=self.lb[:], scalar1=0.0, scalar2=1.0 - 1e-6, op0=ALU.max, op1=ALU.min),
                 reads=["lb"], writes=["lb"])
            P.op("dve", lambda e: e.tensor_scalar(out=self.oml[:], in0=self.lb[:], scalar1=-1.0, scalar2=1.0, op0=ALU.mult, op1=ALU.add),
                 reads=["lb"], writes=["oml"])
            P.emit(nc, st)

    def phase_norm(self, gain, tag, final=False):
        with ExitStack() as st:
            P = Prog()
            self.norm_ops(P, st, gain, final)
            P.emit(self.nc, st)

    def norm_ops(self, P, st, gain, final=False):
        nc, T, TT = self.nc, self.T, self.TT
        x, hT = self.x, self.hT
        if True:
            gb = self.sb(st, "gb", [128, D], F32)
            ss = self.sb(st, "ss", [128, TT], F32)
            rstd = self.sb(st, "rstd", [128, TT], F32)
            junk = self.sb(st, "junk", [128, D], BF16)
            nb = 2
            if final:
                xn = [self.sb(st, "yo%d" % i, [128, D], F32) for i in range(nb)]
            else:
                xn = [self.sb(st, "xn%d" % i, [128, D], BF16) for i in range(nb)]
            P.op("sp", lambda e: e.dma_start(out=gb[:], in_=gain.partition_broadcast(128)), writes=["gb"], chan="gb")
            P.op("dve", lambda e: e.memset(ss[:], 0.0), writes=["ss"])
            outv = self.out.rearrange("(tt p) d -> p tt d", p=128)
            def stats(tt):
                P.op("act", lambda e, tt=tt: e.activation(out=junk[:], in_=x[:, tt, :], func=AF.Square, accum_out=ss[:, tt:tt + 1]),
                     reads=["ss", ("x", tt, 0), ("x", tt, 1)], writes=[("ss", tt), "junk"])
                P.op("act", lambda e, tt=tt: e.activation(out=rstd[:, tt:tt + 1], in_=ss[:, tt:tt + 1], func=AF.Ln, bias=self.epsb[:], scale=1.0 / D),
                     reads=[("ss", tt)], writes=[("rstd", tt)])
                P.op("act", lambda e, tt=tt: e.activation(out=rstd[:, tt:tt + 1], in_=rstd[:, tt:tt + 1], func=AF.Exp, scale=-0.5),
                     reads=[("rstd", tt)], writes=[("rstd", tt)])
            def rest(tt):
                b = tt % nb
                P.op("dve", lambda e, tt=tt, b=b: e.scalar_tensor_tensor(out=xn[b][:], in0=x[:, tt, :], scalar=rstd[:, tt:tt + 1], in1=gb[:],
                                                                          op0=ALU.mult, op1=ALU.mult),
                     reads=[("rstd", tt), "gb", ("x", tt, 0), ("x", tt, 1)], writes=[("xn", b)])
                if final:
                    P.op("sp", lambda e, tt=tt, b=b: e.dma_start(out=outv[:, tt, :], in_=xn[b][:]), reads=[("xn", b)], writes=[("o", tt)],
                         chan="o%d" % b)
                else:
                    pb = self.ps[tt % 2]
                    pT = pb.bitcast(BF16)
                    for kc in range(8):
                        P.op("pe", lambda e, kc=kc, b=b, pT=pT: e.transpose(out=pT[:, kc * 128:(kc + 1) * 128], in_=xn[b][:, kc * 128:(kc + 1) * 128],
                                                                             identity=self.identb[:]),
                             reads=[("xn", b)], writes=[("ps", tt % 2)], signal=(kc == 7))
                    eng = "act" if tt % 2 == 0 else "dve"
                    if eng == "act":
                        P.op("act", lambda e, tt=tt, pT=pT: e.copy(out=hT[:, :, tt * 128:(tt + 1) * 128], in_=pT.rearrange("p (k c) -> p k c", c=128)),
                             reads=[("ps", tt % 2)], writes=[("hT", tt)])
                    else:
                        P.op("dve", lambda e, tt=tt, pT=pT: e.tensor_copy(out=hT[:, :, tt * 128:(tt + 1) * 128], in_=pT.rearrange("p (k c) -> p k c", c=128)),
                             reads=[("ps", tt % 2)], writes=[("hT", tt)])
            for tt in range(TT + 1):
                if tt < TT:
                    stats(tt)
                if tt >= 1:
                    rest(tt - 1)
            if final:
                P.op("sp", None, reads=[("o", tt) for tt in range(TT)])

    def phase_ffn(self, l, which, with_wout=False):
        nc, T, TT, TB, NTB = self.nc, self.T, self.TT, self.TB, self.NTB
        x, hT, ps = self.x, self.hT, self.ps
        wg_d = self.d[which + "_w_gate"][l].rearrange("(kc p) n -> p kc n", p=128)
        wu_d = self.d[which + "_w_up"][l].rearrange("(kc p) n -> p kc n", p=128)
        wd_d = self.d[which + "_w_down"][l].rearrange("(j p) n -> p j n", p=128)
        G = 2
        NG = DFF // (128 * G)
        with ExitStack() as st:
            P = Prog()
            wg = [self.sb(st, "wg%d" % i, [128, 8, 128 * G], BF16) for i in range(2)]
            wu = [self.sb(st, "wu%d" % i, [128, 8, 128 * G], BF16) for i in range(2)]
            wd = [self.sb(st, "wd%d" % i, [128, G, D], BF16) for i in range(2)]
            aT = [self.sb(st, "aT%d" % i, [128, G, T], BF16) for i in range(2)]
            sg = [self.sb(st, "sg%d" % i, [128, TB], BF16) for i in range(2)]
            cnt = {"gu": 0, "d": 0}

            def load(g):
                s = g % 2
                c0 = g * 128 * G
                P.op("pool", lambda e: e.dma_start(out=wg[s][:], in_=wg_d[:, :, c0:c0 + 128 * G]), writes=[("wg", s)], chan="wg%d" % s)
                P.op("pool", lambda e: e.dma_start(out=wu[s][:], in_=wu_d[:, :, c0:c0 + 128 * G]), writes=[("wu", s)], chan="wu%d" % s)
                P.op("pool", lambda e: e.dma_start(out=wd[s][:], in_=wd_d[:, g * G:(g + 1) * G, :]), writes=[("wd", s)], chan="wd%d" % s)

            def GU(g):
                s = g % 2
                for j in range(G):
                    for tb in range(NTB):
                        c = cnt["gu"]
                        cnt["gu"] += 1
                        pg, pu = ps[c % 2], ps[2 + c % 2]
                        tsl = slice(tb * TB, (tb + 1) * TB)
                        htk = [("hT", t) for t in range(tb * TB // 128, (tb + 1) * TB // 128)]
                        for kc in range(8):
                            P.op("pe", lambda e, kc=kc, pg=pg, tsl=tsl: e.matmul(pg[:, 0:TB], lhsT=wg[s][:, kc, j * 128:(j + 1) * 128], rhs=hT[:, kc, tsl],
                                                                                 start=(kc == 0), stop=(kc == 7)),
                                 reads=[("wg", s)] + htk, writes=[("ps", c % 2)], signal=(kc == 7))
                        for kc in range(8):
                            P.op("pe", lambda e, kc=kc, pu=pu, tsl=tsl: e.matmul(pu[:, 0:TB], lhsT=wu[s][:, kc, j * 128:(j + 1) * 128], rhs=hT[:, kc, tsl],
                                                                                 start=(kc == 0), stop=(kc == 7)),
                                 reads=[("wu", s)] + htk, writes=[("ps", 2 + c % 2)], signal=(kc == 7))
                        P.op("act", lambda e, pg=pg, c=c: e.activation(out=sg[c % 2][:], in_=pg[:, 0:TB], func=AF.Silu),
                             reads=[("ps", c % 2)], writes=[("sg", c % 2)])
                        P.op("dve", lambda e, pu=pu, c=c, tsl=tsl: e.tensor_tensor(out=aT[s][:, j, tsl], in0=sg[c % 2][:], in1=pu[:, 0:TB], op=ALU.mult),
                             reads=[("sg", c % 2), ("ps", 2 + c % 2)], writes=[("aT", s, tb)])

            def DN(g):
                s = g % 2
                for tt in range(TT):
                    tb = (tt * 128) // TB
                    for nh in range(2):
                        c = cnt["d"]
                        cnt["d"] += 1
                        pd = ps[4 + c % 4]
                        for j in range(G):
                            P.op("pe", lambda e, j=j, pd=pd, tt=tt, nh=nh: e.matmul(pd[:], lhsT=aT[s][:, j, tt * 128:(tt + 1) * 128],
                                                                                     rhs=wd[s][:, j, nh * 512:(nh + 1) * 512],
                                                                                     start=(j == 0), stop=(j == G - 1)),
                                 reads=[("aT", s, tb), ("wd", s)], writes=[("ps", 4 + c % 4)], signal=(j == G - 1))
                        P.op("dve", lambda e, pd=pd, tt=tt, nh=nh: e.scalar_tensor_tensor(out=x[:, tt, nh * 512:(nh + 1) * 512], in0=pd[:], scalar=0.5,
                                                                                          in1=x[:, tt, nh * 512:(nh + 1) * 512], op0=ALU.mult, op1=ALU.add),
                             reads=[("ps", 4 + c % 4)], writes=[("x", tt, nh)])

            if with_wout:
                self.wout_ops(P, st, l)
            load(0)
            load(1)
            self.norm_ops(P, st, self.d[which + "_norm"][l])
            GU(0)
            for g in range(1, NG):
                GU(g)
                DN(g - 1)
                if g + 1 < NG:
                    load(g + 1)
            DN(NG - 1)
            P.emit(nc, st)

    def phase_attn(self, l, kind):
        nc, T, TT, TB, NTB = self.nc, self.T, self.TT, self.TB, self.NTB
        x, hT, ps, oT = self.x, self.hT, self.ps, self.oT
        win = self.d["w_in"][l].rearrange("(kc p) n -> p kc n", p=128)
        fox = kind == "fox"
        qc0 = 0 if fox else 772
        vc0 = 512 if fox else 1284
        nv = 260 if fox else 256
        obase = 0 if fox else 2
        with ExitStack() as st:
            P = Prog()
            wqk = self.sb(st, "wqk", [128, 8, 512], BF16)
            wv = self.sb(st, "wv", [128, 8, nv], BF16)
            qT = self.sb(st, "qT", [128, 2, T], BF16)
            kT = self.sb(st, "kT", [128, 2, T], BF16)
            va = self.sb(st, "va", [128, TT, 4, 72], BF16)
            otok = self.sb(st, "otok", [128, TT, 256], BF16)
            NPB = 6
            pt = [self.sb(st, "pt%d" % i, [128, 512], BF16) for i in range(NPB)]
            rcp = [self.sb(st, "rcp%d" % i, [128, 1], F32) for i in range(2)]
            P.op("pool", lambda e: e.dma_start(out=wqk[:], in_=win[:, :, qc0:qc0 + 512]), writes=["wqk"], chan="wqk")
            P.op("pool", lambda e: e.dma_start(out=wv[:], in_=win[:, :, vc0:vc0 + nv]), writes=["wv"], chan="wv")
            P.op("pool", lambda e: e.memset(va[:, :, :, 64:65], 1.0), writes=["va1"])
            if fox:
                self.norm_ops(P, st, self.d["mix_norm"][l])
            if fox:
                lfr = self.sb(st, "lfr", [128, TT, 4], F32)
                cc = self.sb(st, "cc", [128, TT, 4], F32)
                carry = self.sb(st, "carry", [128, TT, 4], F32)
                tot = self.sb(st, "tot", [128, TT, 4], F32)
                cmid = self.sb(st, "cmid", [128, TT, 4], F32)
                bias = self.sb(st, "biasf", [128, 4, TT, TT], F32)
                wexp = bias
                vs = [self.sb(st, "vs%d" % i, [128, 4, 72], BF16) for i in range(NPB)]
                P.op("pool", lambda e: e.memset(bias[:], 0.0), writes=["bias"])
                cn = 0
                for wi, dst in ((0, qT), (1, kT)):
                    for ch in range(2):
                        for tb in range(NTB):
                            pb = ps[cn % 2]
                            tsl = slice(tb * TB, (tb + 1) * TB)
                            for kc in range(8):
                                P.op("pe", lambda e, kc=kc, pb=pb, tsl=tsl, wi=wi, ch=ch: e.matmul(
                                    pb[:, 0:TB], lhsT=wqk[:, kc, wi * 256 + ch * 128: wi * 256 + (ch + 1) * 128], rhs=hT[:, kc, tsl],
                                    start=(kc == 0), stop=(kc == 7)), reads=["wqk"] + [("hT", t) for t in range(tb * TB // 128, (tb + 1) * TB // 128)],
                                     writes=[("ps", cn % 2)], signal=(kc == 7))
                            if cn % 2 == 0:
                                P.op("act", lambda e, pb=pb, dst=dst, ch=ch, tsl=tsl: e.copy(out=dst[:, ch, tsl], in_=pb[:, 0:TB]),
                                     reads=[("ps", cn % 2)], writes=[("qk", wi, t) for t in range(tb * TB // 128, (tb + 1) * TB // 128)])
                            else:
                                P.op("dve", lambda e, pb=pb, dst=dst, ch=ch, tsl=tsl: e.tensor_copy(out=dst[:, ch, tsl], in_=pb[:, 0:TB]),
                                     reads=[("ps", cn % 2)], writes=[("qk", wi, t) for t in range(tb * TB // 128, (tb + 1) * TB // 128)])
                            cn += 1
            else:
                qkt = [self.sb(st, "qkt%d" % i, [128, 8, 64], BF16) for i in range(2)]
                rt = [self.sb(st, "rt%d" % i, [128, 8, 8], F32) for i in range(4)]
                for tt in range(TT):
                    pb = ps[tt % 2]
                    b = tt % 2
                    for kc in range(8):
                        P.op("pe", lambda e, kc=kc, pb=pb, tt=tt: e.matmul(pb[:], lhsT=hT[:, kc, tt * 128:(tt + 1) * 128], rhs=wqk[:, kc, :],
                                                                           start=(kc == 0), stop=(kc == 7)),
                             reads=["wqk"], writes=[("ps", tt % 2)], signal=(kc == 7))
                    pv = pb[:].rearrange("p (h d) -> p h d", d=64)
                    cb = self.cosb[:, tt:tt + 1, :].broadcast_to([128, 8, 8])
                    sbb = self.sinb[:, tt:tt + 1, :].broadcast_to([128, 8, 8])
                    rk = [("rt", i) for i in range(4)]
                    P.op("dve", lambda e, pv=pv, cb=cb: e.tensor_tensor(out=rt[0][:], in0=pv[:, :, 0:8], in1=cb, op=ALU.mult), reads=[("ps", tt % 2)], writes=[rk[0]])
                    P.op("dve", lambda e, pv=pv, sbb=sbb: e.tensor_tensor(out=rt[1][:], in0=pv[:, :, 8:16], in1=sbb, op=ALU.mult), reads=[("ps", tt % 2)], writes=[rk[1]])
                    P.op("dve", lambda e, pv=pv, cb=cb: e.tensor_tensor(out=rt[2][:], in0=pv[:, :, 8:16], in1=cb, op=ALU.mult), reads=[("ps", tt % 2)], writes=[rk[2]])
                    P.op("dve", lambda e, pv=pv, sbb=sbb: e.tensor_tensor(out=rt[3][:], in0=pv[:, :, 0:8], in1=sbb, op=ALU.mult), reads=[("ps", tt % 2)], writes=[rk[3]])
                    P.op("dve", lambda e, b=b: e.tensor_tensor(out=qkt[b][:, :, 0:8], in0=rt[0][:], in1=rt[1][:], op=ALU.subtract),
                         reads=[rk[0], rk[1]], writes=[("qkt", b)])
                    P.op("dve", lambda e, b=b: e.tensor_tensor(out=qkt[b][:, :, 8:16], in0=rt[2][:], in1=rt[3][:], op=ALU.add),
                         reads=[rk[2], rk[3]], writes=[("qkt", b)])
                    P.op("dve", lambda e, b=b, pv=pv: e.tensor_copy(out=qkt[b][:, :, 16:64], in_=pv[:, :, 16:64]), reads=[("ps", tt % 2)], writes=[("qkt", b)])
                    pT = ps[2 + tt % 2].bitcast(BF16)
                    qf = qkt[b][:].rearrange("p h d -> p (h d)")
                    for c4 in range(4):
                        P.op("pe", lambda e, c4=c4, pT=pT, qf=qf: e.transpose(out=pT[:, c4 * 128:(c4 + 1) * 128], in_=qf[:, c4 * 128:(c4 + 1) * 128],
                                                                               identity=self.identb[:]),
                             reads=[("qkt", b)], writes=[("ps", 2 + tt % 2)], signal=(c4 == 3))
                    P.op("act", lambda e, pT=pT, tt=tt: e.copy(out=qT[:, :, tt * 128:(tt + 1) * 128], in_=pT[:, 0:256].rearrange("p (k c) -> p k c", c=128)),
                         reads=[("ps", 2 + tt % 2)], writes=[("qk", 0, tt)])
                    P.op("act", lambda e, pT=pT, tt=tt: e.copy(out=kT[:, :, tt * 128:(tt + 1) * 128], in_=pT[:, 256:512].rearrange("p (k c) -> p k c", c=128)),
                         reads=[("ps", 2 + tt % 2)], writes=[("qk", 1, tt)])
            for tt in range(TT):
                pb = ps[4 + tt % 2]
                for kc in range(8):
                    P.op("pe", lambda e, kc=kc, pb=pb, tt=tt: e.matmul(pb[:, 0:nv], lhsT=hT[:, kc, tt * 128:(tt + 1) * 128], rhs=wv[:, kc, :],
                                                                       start=(kc == 0), stop=(kc == 7)),
                         reads=["wv", ("hT", tt)], writes=[("ps", 4 + tt % 2)], signal=(kc == 7))
                P.op("act", lambda e, pb=pb, tt=tt: e.copy(out=va[:, tt, :, 0:64], in_=pb[:, 0:256].rearrange("p (h d) -> p h d", d=64)),
                     reads=[("ps", 4 + tt % 2)], writes=[("va", tt)])
                if fox:
                    P.op("dve", lambda e, pb=pb, tt=tt: e.tensor_tensor(out=lfr[:, tt, :], in0=pb[:, 256:260], in1=self.fbias[:, l * 4:(l + 1) * 4], op=ALU.add),
                         reads=[("ps", 4 + tt % 2), ("va", tt)], writes=["lfr"])
            if fox:
                lf2 = lfr[:].rearrange("p t h -> p (t h)")
                N4 = TT * 4
                P.op("act", lambda e: e.activation(out=lf2, in_=lf2, func=AF.Sigmoid), reads=["lfr"], writes=["lfr"])
                P.op("act", lambda e: e.activation(out=lf2, in_=lf2, func=AF.Ln), reads=["lfr"], writes=["lfr"])
                P.op("pe", lambda e: e.matmul(ps[6][:, 0:N4], lhsT=self.trif, rhs=lf2, start=True, stop=True), reads=["lfr"], writes=[("ps", 6)])
                P.op("pe", lambda e: e.matmul(ps[7][:, 0:N4], lhsT=self.onesf, rhs=lf2, start=True, stop=True), reads=["lfr"], writes=[("ps", 7)])
                P.op("dve", lambda e: e.tensor_copy(out=tot[:].rearrange("p t h -> p (t h)"), in_=ps[7][:, 0:N4]), reads=[("ps", 7)], writes=["tot"])
                P.op("dve", lambda e: e.memset(carry[:, 0, :], 0.0), writes=["carry"])
                for tt in range(1, TT):
                    P.op("dve", lambda e, tt=tt: e.tensor_tensor(out=carry[:, tt, :], in0=carry[:, tt - 1, :], in1=tot[:, tt - 1, :], op=ALU.add),
                         reads=["carry", "tot"], writes=["carry"])
                P.op("dve", lambda e: e.tensor_tensor(out=cc[:].rearrange("p t h -> p (t h)"), in0=ps[6][:, 0:N4], in1=carry[:].rearrange("p t h -> p (t h)"), op=ALU.add),
                     reads=[("ps", 6), "carry"], writes=["cc"])
                P.op("pe", lambda e: e.matmul(ps[7][:, 0:N4], lhsT=self.e63f, rhs=cc[:].rearrange("p t h -> p (t h)"), start=True, stop=True),
                     reads=["cc"], writes=[("ps", 7)])
                P.op("dve", lambda e: e.tensor_copy(out=cmid[:].rearrange("p t h -> p (t h)"), in_=ps[7][:, 0:N4]), reads=[("ps", 7)], writes=["cmid"])
                cmv = cmid[:].rearrange("p t h -> p h t")
                for kb in range(TT):
                    for h in range(4):
                        P.op("dve", lambda e, kb=kb, h=h: e.tensor_scalar(out=bias[:, h, kb, kb:TT], in0=cmv[:, h, kb:TT], scalar1=cc[:, kb, h:h + 1], scalar2=None,
                                                                          op0=ALU.subtract), reads=["cmid", "cc"], writes=["bias"])
                P.op("act", lambda e: e.activation(out=wexp[:].rearrange("p h a b -> p (h a b)"), in_=bias[:].rearrange("p h a b -> p (h a b)"), func=AF.Exp),
                     reads=["bias"], writes=["bias", "wexp"])
            items = []
            oc = 0
            for h in range(4):
                for qs in range(TT):
                    groups = [list(range(k0, min(k0 + 4, qs + 1))) for k0 in range(0, qs + 1, 4)]
                    for grp in groups:
                        items.append((h, qs, grp, oc, len(items)))
                    oc += 1

            def front(it):
                h, qs, grp, oc, i_ = it
                ch, po = h // 2, 64 * (h % 2)
                pS = ps[i_ % 6]
                skey = ("ps", i_ % 6)
                pb_i = i_ % NPB
                ng = len(grp)
                qsl = slice(qs * 128, (qs + 1) * 128)
                for i, kb in enumerate(grp):
                    P.op("pe", lambda e: e.matmul(pS[:, i * 128:(i + 1) * 128], lhsT=kT[po:po + 64, ch, kb * 128:(kb + 1) * 128],
                                                  rhs=qT[po:po + 64, ch, qsl], start=True, stop=True),
                         reads=[("qk", 1, kb), ("qk", 0, qs)], writes=[skey], signal=(i == ng - 1))
                P.op("act", lambda e: e.activation(out=pt[pb_i][:, 0:ng * 128], in_=pS[:, 0:ng * 128], func=AF.Exp, scale=0.125),
                     reads=[skey], writes=[("pt", pb_i)])
                if fox:
                    k0 = grp[0]
                    P.op("dve", lambda e: e.tensor_tensor(out=vs[pb_i][:, 0:ng, 0:65], in0=va[:, k0:k0 + ng, h, 0:65],
                                                          in1=wexp[:, h, k0:k0 + ng, qs].unsqueeze(2).broadcast_to([128, ng, 65]), op=ALU.mult),
                         reads=["wexp", "va1"] + [("va", kb) for kb in grp], writes=[("vs", pb_i)])
                    if grp[-1] == qs:
                        dsl = slice((ng - 1) * 128, ng * 128)
                        P.op("dve", lambda e: e.tensor_tensor(out=pt[pb_i][:, dsl], in0=pt[pb_i][:, dsl], in1=self.trib[:], op=ALU.mult),
                             reads=[("pt", pb_i)], writes=[("pt", pb_i)])
                else:
                    sl0 = 15 - qs + grp[0]
                    P.op("dve", lambda e: e.tensor_tensor(out=pt[pb_i][:, 0:ng * 128], in0=pt[pb_i][:, 0:ng * 128],
                                                          in1=self.mtb[:, sl0 * 128:(sl0 + ng) * 128], op=ALU.mult),
                         reads=[("pt", pb_i)], writes=[("pt", pb_i)])

            def back(it):
                h, qs, grp, oc, i_ = it
                pO = ps[6 + oc % 2]
                okey = ("ps", 6 + oc % 2)
                pb_i = i_ % NPB
                for i, kb in enumerate(grp):
                    rhs = vs[pb_i][:, i, 0:65] if fox else va[:, kb, h, 0:65]
                    P.op("pe", lambda e: e.matmul(pO[:, 0:65], lhsT=pt[pb_i][:, i * 128:(i + 1) * 128], rhs=rhs,
                                                  start=(kb == 0), stop=(kb == qs)),
                         reads=[("pt", pb_i), ("va", kb), "va1", ("vs", pb_i)], writes=[okey], signal=(kb == qs))
                if grp[-1] == qs:
                    r = rcp[oc % 2]
                    P.op("dve", lambda e: e.reciprocal(out=r[:], in_=pO[:, 64:65]), reads=[okey], writes=[("rcp", oc % 2)])
                    P.op("act", lambda e: e.activation(out=otok[:, qs, h * 64:(h + 1) * 64], in_=pO[:, 0:64], func=AF.Copy, scale=r[:]),
                         reads=[okey, ("rcp", oc % 2)], writes=[("otok", qs)])

            LA = 4
            for i in range(len(items) + LA):
                if i < len(items):
                    front(items[i])
                if i - LA >= 0:
                    back(items[i - LA])
            for tt in range(TT):
                pT = ps[4 + tt % 2].bitcast(BF16)
                for c2 in range(2):
                    P.op("pe", lambda e, c2=c2, pT=pT, tt=tt: e.transpose(out=pT[:, c2 * 128:(c2 + 1) * 128], in_=otok[:, tt, c2 * 128:(c2 + 1) * 128],
                                                                           identity=self.identb[:]),
                         reads=[("otok", tt)], writes=[("ps", 4 + tt % 2)], signal=(c2 == 1))
                P.op("dve", lambda e, pT=pT, tt=tt: e.tensor_copy(out=oT[:, obase:obase + 2, tt * 128:(tt + 1) * 128],
                                                                    in_=pT[:, 0:256].rearrange("p (k c) -> p k c", c=128)),
                     reads=[("ps", 4 + tt % 2)], writes=[("oT", tt)])
            P.emit(nc, st)

    def phase_hgrn(self, l):
        nc, T, TT, TB, NTB, NCH = self.nc, self.T, self.TT, self.TB, self.NTB, self.NCH
        x, hT, ps, oT = self.x, self.hT, self.ps, self.oT
        win = self.d["w_in"][l].rearrange("(kc p) n -> p kc n", p=128)
        CPB = TB // 64
        with ExitStack() as st:
            P = Prog()
            w4 = [self.sb(st, "w4_%d" % i, [128, 8, 4, 128], BF16) for i in range(2)]
            A2 = [self.sb(st, "hA%d" % i, [128, TB], F32) for i in range(2)]
            C2 = [self.sb(st, "hC%d" % i, [128, TB], F32) for i in range(2)]
            D2 = [self.sb(st, "hD%d" % i, [128, TB], F32) for i in range(2)]
            Q2 = [self.sb(st, "hqs%d" % i, [128, TB], BF16) for i in range(2)]
            qt2 = [self.sb(st, "hqt%d" % i, [128, TB], BF16) for i in range(2)]
            kt2 = [self.sb(st, "hkt%d" % i, [128, TB], BF16) for i in range(2)]
            vt2 = [self.sb(st, "hv%d" % i, [64, CPB, 128], BF16) for i in range(2)]
            mm2 = [self.sb(st, "hm%d" % i, [128, CPB], F32) for i in range(2)]
            dd2 = [self.sb(st, "hd%d" % i, [128, CPB], F32) for i in range(2)]
            rr2 = [self.sb(st, "hr%d" % i, [128, CPB], F32) for i in range(2)]
            rmask = self.sb(st, "rmask", [128, TB], F32)
            S2 = [self.sb(st, "hS%d" % i, [128, 128], F32) for i in range(2)]
            Mc = [self.sb(st, "hMc%d" % i, [128, 128], BF16) for i in range(2)]
            tmpkv = self.sb(st, "hkv", [128, CPB, 128], F32)
            ktok = [self.sb(st, "hktok%d" % i, [64, CPB, 128], BF16) for i in range(2)]
            atm = [self.sb(st, "hatm%d" % i, [64, CPB, 64], BF16) for i in range(2)]
            sq = self.sb(st, "hsq", [128, TB], BF16)
            rs = self.sb(st, "hrs", [128, TB], F32)
            sgt = self.sb(st, "hsg", [128, TB], BF16)
            P.op("pool", lambda e: e.memset(rmask[:], 1.0), writes=["rmask"])
            P.op("pool", lambda e: e.memset(rmask[:].rearrange("p (c t) -> p c t", t=64)[:, :, 0:1], 0.0), writes=["rmask"])

            def loadw(h):
                s = h % 2
                for i, c0 in enumerate((1540, 2052, 2564, 3076)):
                    P.op("pool", lambda e: e.dma_start(out=w4[s][:, :, i, :], in_=win[:, :, c0 + h * 128: c0 + (h + 1) * 128]),
                         writes=[("w4", s, i)], chan="w4_%d_%d" % (s, i))

            steps = [(h, tb) for h in range(4) for tb in range(NTB)]
            N = len(steps)
            cnt = {"pc": 0, "vc": 0}
            pA = ps[4]
            pK = ps[5].bitcast(BF16)

            def stage1(i):
                h, tb = steps[i]
                s, p2 = h % 2, i % 2
                lbc = self.lb[:, l * 4 + h: l * 4 + h + 1]
                omc = self.oml[:, l * 4 + h: l * 4 + h + 1]
                tsl = slice(tb * TB, (tb + 1) * TB)
                A, C, Dd, qs_ = A2[p2], C2[p2], D2[p2], Q2[p2]
                qt, kt, vtok, mm, dd, rr = qt2[p2], kt2[p2], vt2[p2], mm2[p2], dd2[p2], rr2[p2]
                kA, kC, kD, kQ = ("hA", p2), ("hC", p2), ("hD", p2), ("hQ", p2)
                for (wi, dst, fn, kk_) in ((1, A, AF.Sigmoid, kA), (0, qs_, AF.Silu, kQ)):
                    pb = ps[cnt["pc"] % 2]
                    pk = ("ps", cnt["pc"] % 2)
                    cnt["pc"] += 1
                    for kc in range(8):
                        P.op("pe", lambda e: e.matmul(pb[:, 0:TB], lhsT=w4[s][:, kc, wi, :], rhs=hT[:, kc, tsl], start=(kc == 0), stop=(kc == 7)),
                             reads=[("w4", s, wi)], writes=[pk], signal=(kc == 7))
                    P.op("act", lambda e: e.activation(out=dst[:], in_=pb[:, 0:TB], func=fn), reads=[pk], writes=[kk_])
                vbanks = []
                for half in range((CPB + 3) // 4):
                    n4 = min(4, CPB - half * 4)
                    pb = ps[cnt["pc"] % 2]
                    pk = ("ps", cnt["pc"] % 2)
                    cnt["pc"] += 1
                    vbanks.append((pb, pk, n4))
                    for j in range(n4):
                        c = tb * CPB + half * 4 + j
                        for kc in range(8):
                            P.op("pe", lambda e: e.matmul(pb[0:64, j * 128:(j + 1) * 128], lhsT=hT[:, kc, c * 64:(c + 1) * 64], rhs=w4[s][:, kc, 2, :],
                                                          start=(kc == 0), stop=(kc == 7)),
                                 reads=[("w4", s, 2)], writes=[pk], signal=(kc == 7 and j == n4 - 1))
                P.op("dve", lambda e: e.tensor_scalar(out=A[:], in0=A[:], scalar1=omc, scalar2=lbc, op0=ALU.mult, op1=ALU.add), reads=[kA], writes=[kA])
                P.op("act", lambda e: e.activation(out=C[:], in_=A[:], func=AF.Ln), reads=[kA], writes=[kC])
                P.op("dve", lambda e: e.tensor_scalar(out=A[:], in0=A[:], scalar1=-1.0, scalar2=1.0, op0=ALU.mult, op1=ALU.add), reads=[kA], writes=[kA])
                P.op("dve", lambda e: e.tensor_tensor_scan(out=Dd[:], data0=rmask[:], data1=C[:], initial=0.0, op0=ALU.mult, op1=ALU.add),
                     reads=[kC, "rmask"], writes=[kD])
                d3 = Dd[:].rearrange("p (c t) -> p c t", t=64)
                c3 = C[:].rearrange("p (c t) -> p c t", t=64)
                P.op("dve", lambda e: e.tensor_tensor(out=c3, in0=d3, in1=d3[:, :, 31:32].broadcast_to([128, CPB, 64]), op=ALU.subtract),
                     reads=[kD], writes=[kC])
                P.op("act", lambda e: e.activation(out=mm[:], in_=d3[:, :, 31], func=AF.Exp), reads=[kD], writes=[("mm", p2)])
                P.op("act", lambda e: e.activation(out=dd[:], in_=d3[:, :, 63], func=AF.Exp), reads=[kD], writes=[("dd", p2)])
                P.op("act", lambda e: e.activation(out=rr[:], in_=c3[:, :, 63], func=AF.Exp), reads=[kC], writes=[("rr", p2)])
                P.op("act", lambda e: e.activation(out=Dd[:], in_=C[:], func=AF.Exp), reads=[kC], writes=[kD])
                P.op("act", lambda e: e.activation(out=C[:], in_=C[:], func=AF.Exp, scale=-1.0), reads=[kC], writes=[kC])
                P.op("dve", lambda e: e.tensor_tensor(out=qt[:], in0=qs_[:], in1=Dd[:], op=ALU.mult), reads=[kQ, kD], writes=[("qt", p2)])
                P.op("dve", lambda e: e.tensor_tensor(out=kt[:], in0=A[:], in1=C[:], op=ALU.mult), reads=[kA, kC], writes=[("kt", p2)])
                for half, (pb, pk, n4) in enumerate(vbanks):
                    if half % 2 == 0:
                        P.op("act", lambda e: e.copy(out=vtok[:, half * 4:half * 4 + n4, :], in_=pb[0:64, 0:n4 * 128].rearrange("p (c v) -> p c v", v=128)),
                             reads=[pk], writes=[("v", p2, half)])
                    else:
                        P.op("dve", lambda e: e.tensor_copy(out=vtok[:, half * 4:half * 4 + n4, :], in_=pb[0:64, 0:n4 * 128].rearrange("p (c v) -> p c v", v=128)),
                             reads=[pk], writes=[("v", p2, half)])

            def passA_pe(i):
                p2 = i % 2
                qt, kt, vtok = qt2[p2], kt2[p2], vt2[p2]
                for cc in range(CPB):
                    cs = slice(cc * 64, (cc + 1) * 64)
                    P.op("pe", lambda e: e.matmul(pA[0:64, cc * 64:(cc + 1) * 64], lhsT=kt[:, cs], rhs=qt[:, cs], start=True, stop=True),
                         reads=[("kt", p2), ("qt", p2)], writes=[("ps", 4)], signal=(cc == CPB - 1))
                for cc in range(CPB):
                    cs = slice(cc * 64, (cc + 1) * 64)
                    P.op("pe", lambda e: e.transpose(out=pK[0:64, cc * 128:(cc + 1) * 128], in_=kt[:, cs], identity=self.identb[:]),
                         reads=[("kt", p2)], writes=[("ps", 5)], signal=(cc == CPB - 1))
                P.op("dve", lambda e: e.tensor_tensor(out=atm[p2][:], in0=pA[0:64, 0:CPB * 64].rearrange("p (c t) -> p c t", t=64),
                                                      in1=self.trib[0:64, 0:64].unsqueeze(1).broadcast_to([64, CPB, 64]), op=ALU.mult),
                     reads=[("ps", 4)], writes=[("atm", p2)])
                P.op("act", lambda e: e.copy(out=ktok[p2][:], in_=pK[0:64, 0:CPB * 128].rearrange("p (c t) -> p c t", t=128)),
                     reads=[("ps", 5)], writes=[("ktok", p2)])
                for cc in range(CPB):
                    bk = 6 + cc // 4
                    P.op("pe", lambda e: e.matmul(ps[bk][:, (cc % 4) * 128:(cc % 4 + 1) * 128], lhsT=ktok[p2][:, cc, :], rhs=vtok[:, cc, :], start=True, stop=True),
                         reads=[("ktok", p2), ("v", p2, cc // 4)], writes=[("ps", bk)], signal=(cc % 4 == 3 or cc == CPB - 1))

            def passA_evac(i):
                p2 = i % 2
                rr = rr2[p2]
                for half in range((CPB + 3) // 4):
                    n4 = min(4, CPB - half * 4)
                    P.op("dve", lambda e: e.tensor_tensor(out=tmpkv[:, half * 4:half * 4 + n4, :],
                                                          in0=ps[6 + half][:, 0:n4 * 128].rearrange("p (c v) -> p c v", v=128),
                                                          in1=rr[:, half * 4:half * 4 + n4].unsqueeze(2).broadcast_to([128, n4, 128]), op=ALU.mult),
                         reads=[("ps", 6 + half), ("rr", p2)], writes=[("tmpkv", half)])

            def passB(i):
                h, tb = steps[i]
                p2 = i % 2
                qt, vtok, mm, dd = qt2[p2], vt2[p2], mm2[p2], dd2[p2]
                pO = ps[2 + p2]
                if tb == 0:
                    P.op("dve", lambda e: e.memset(S2[0][:], 0.0), writes=[("S", 0)])
                for cc in range(CPB):
                    cs = slice(cc * 64, (cc + 1) * 64)
                    b2 = cc % 2
                    co = cc * 64
                    cur, nxt = S2[cc % 2], S2[(cc + 1) % 2]
                    P.op("act", lambda e: e.activation(out=Mc[b2][:], in_=cur[:], func=AF.Copy, scale=mm[:, cc:cc + 1]),
                         reads=[("S", cc % 2), ("mm", p2)], writes=[("Mc", b2)])
                    P.op("dve", lambda e: e.scalar_tensor_tensor(out=nxt[:], in0=cur[:], scalar=dd[:, cc:cc + 1], in1=tmpkv[:, cc, :], op0=ALU.mult, op1=ALU.add),
                         reads=[("S", cc % 2), ("dd", p2), ("tmpkv", cc // 4)], writes=[("S", (cc + 1) % 2)])
                    P.op("pe", lambda e: e.matmul(pO[:, co:co + 64], lhsT=Mc[b2][:], rhs=qt[:, cs], start=True, stop=False),
                         reads=[("Mc", b2), ("qt", p2)], writes=[("ps", 2 + p2)], signal=False)
                    P.op("pe", lambda e: e.matmul(pO[:, co:co + 64], lhsT=vtok[:, cc, :], rhs=atm[p2][:, cc, :], start=False, stop=True),
                         reads=[("atm", p2), ("v", p2, cc // 4)], writes=[("ps", 2 + p2)])

            def epilogue(i):
                h, tb = steps[i]
                s, p2 = h % 2, i % 2
                nwc = self.nw[:, l * 4 + h: l * 4 + h + 1]
                pO = ps[2 + p2]
                tsl = slice(tb * TB, (tb + 1) * TB)
                okeys = [("ps", 2 + p2)]
                P.op("act", lambda e: e.activation(out=sq[:], in_=pO[:, 0:TB], func=AF.Square), reads=okeys, writes=["sq"])
                P.op("pe", lambda e: e.matmul(ps[4][:, 0:TB], lhsT=self.onesb[:], rhs=sq[:], start=True, stop=True), reads=["sq"], writes=[("ps", 4)])
                P.op("act", lambda e: e.activation(out=rs[:], in_=ps[4][:, 0:TB], func=AF.Ln, bias=self.epsb[:], scale=1.0 / 128), reads=[("ps", 4)], writes=["rs"])
                P.op("act", lambda e: e.activation(out=rs[:], in_=rs[:], func=AF.Exp, scale=-0.5), reads=["rs"], writes=["rs"])
                pg = ps[cnt["pc"] % 2]
                pk = ("ps", cnt["pc"] % 2)
                cnt["pc"] += 1
                for kc in range(8):
                    P.op("pe", lambda e: e.matmul(pg[:, 0:TB], lhsT=w4[s][:, kc, 3, :], rhs=hT[:, kc, tsl], start=(kc == 0), stop=(kc == 7)),
                         reads=[("w4", s, 3)], writes=[pk], signal=(kc == 7))
                P.op("act", lambda e: e.activation(out=sgt[:], in_=pg[:, 0:TB], func=AF.Sigmoid), reads=[pk], writes=["sgt"])
                P.op("dve", lambda e: e.tensor_tensor(out=rs[:], in0=pO[:, 0:TB], in1=rs[:], op=ALU.mult), reads=okeys + ["rs"], writes=["rs"])
                P.op("dve", lambda e: e.scalar_tensor_tensor(out=oT[:, 4 + h, tsl], in0=rs[:], scalar=nwc, in1=sgt[:], op0=ALU.mult, op1=ALU.mult),
                     reads=["rs", "sgt"], writes=[("oTh", h, tb)])

            loadw(0)
            loadw(1)
            stage1(0)
            passA_pe(0)
            passA_evac(0)
            if N > 1:
                stage1(1)
            for i in range(N):
                h, tb = steps[i]
                passB(i)
                epilogue(i)
                if tb == NTB - 1 and h + 2 < 4:
                    loadw(h + 2)
                if i + 1 < N:
                    passA_pe(i + 1)
                    passA_evac(i + 1)
                if i + 2 < N:
                    stage1(i + 2)
            P.emit(nc, st)

    def phase_wout(self, l):
        with ExitStack() as st:
            P = Prog()
            self.wout_ops(P, st, l)
            P.emit(self.nc, st)

    def wout_ops(self, P, st, l):
        nc, T, TT = self.nc, self.T, self.TT
        x, ps, oT = self.x, self.ps, self.oT
        wo_d = self.d["w_out"][l].rearrange("(kc p) n -> p kc n", p=128)
        if True:
            wo = self.sb(st, "wo", [128, 8, D], BF16)
            for half in range(2):
                P.op("pool", lambda e, half=half: e.dma_start(out=wo[:, half * 4:(half + 1) * 4, :], in_=wo_d[:, half * 4:(half + 1) * 4, :]),
                     writes=[("wo", half)], chan="wo%d" % half)
            c = 0
            for tt in range(TT):
                for nh in range(2):
                    pb = ps[4 + c % 4]
                    pk = ("ps", 4 + c % 4)
                    c += 1
                    for kc in range(8):
                        P.op("pe", lambda e, kc=kc, pb=pb, tt=tt, nh=nh: e.matmul(pb[:], lhsT=oT[:, kc, tt * 128:(tt + 1) * 128], rhs=wo[:, kc, nh * 512:(nh + 1) * 512],
                                                                                  start=(kc == 0), stop=(kc == 7)),
                             reads=[("wo", kc // 4)], writes=[pk], signal=(kc == 7))
                    P.op("dve", lambda e, pb=pb, tt=tt, nh=nh: e.tensor_tensor(out=x[:, tt, nh * 512:(nh + 1) * 512], in0=pb[:], in1=x[:, tt, nh * 512:(nh + 1) * 512], op=ALU.add),
                         reads=[pk], writes=[("x", tt, nh)])


_NC_CACHE = {}


def prep_inputs(inputs, b, T, depth):
    f = lambda a: np.ascontiguousarray(np.asarray(a, dtype=np.float32))
    m = dict(
        x=f(inputs["x"][b, :T]), positions=np.ascontiguousarray(np.asarray(inputs["positions"][b, :T], dtype=np.int32)),
        fox_forget_bias=f(inputs["fox_forget_bias"][:depth]).reshape(depth * 4),
        hgrn_lower_bounds=f(inputs["hgrn_lower_bounds"][:depth]).reshape(depth * 4, 128),
        hgrn_out_norm=f(inputs["hgrn_out_norm"][:depth]).reshape(depth * 4, 128),
        final_norm=f(inputs["final_norm"]), consts=make_consts()[0], mtab=make_consts()[1],
    )
    for k in ("ffn1_norm", "ffn1_w_gate", "ffn1_w_up", "ffn1_w_down", "mix_norm", "w_in", "w_out", "ffn2_norm",
              "ffn2_w_gate", "ffn2_w_up", "ffn2_w_down"):
        m[k] = f(inputs[k][:depth])
    return m


def kernel(**inputs):
    B, T = inputs["x"].shape[0], inputs["x"].shape[1]
    depth = inputs["w_in"].shape[0]
    key = (T, depth)
    if key not in _NC_CACHE:
        _NC_CACHE[key] = Builder(T, depth).build()
    nc = _NC_CACHE[key]
    shared = prep_inputs(inputs, 0, T, depth)
    in_maps = []
    for b in range(B):
        m = dict(shared)
        m["x"] = np.ascontiguousarray(np.asarray(inputs["x"][b], dtype=np.float32))
        m["positions"] = np.ascontiguousarray(np.asarray(inputs["positions"][b], dtype=np.int32))
        in_maps.append(m)
    res = run_bass_kernel_spmd(nc, in_maps, core_ids=list(range(B)))
    return np.stack([np.asarray(r["out"], dtype=np.float32) for r in res.results], axis=0)
```

```python
import math
import types
from contextlib import ExitStack

import numpy as np
import concourse.bass as bass
import concourse.mybir as mybir
from concourse.bass_utils import run_bass_kernel_spmd

F32 = mybir.dt.float32
BF16 = mybir.dt.bfloat16
I32 = mybir.dt.int32
AF = mybir.ActivationFunctionType
ALU = mybir.AluOpType
AX = mybir.AxisListType

D = 1024
DFF = 2816
NIN = 3588
EPS = 1e-6
ENGS = ("pe", "act", "dve", "pool", "sp")


def _freeze(fn):
    if fn is None or fn.__closure__ is None:
        return fn
    cells = []
    for c in fn.__closure__:
        try:
            cells.append(types.CellType(c.cell_contents))
        except ValueError:
            cells.append(c)
    return types.FunctionType(fn.__code__, fn.__globals__, fn.__name__, fn.__defaults__, tuple(cells))


class Prog:
    def __init__(self):
        self.ops = {e: [] for e in ENGS}
        self.last_w = {}
        self.readers = {}
        self.chans = []

    def op(self, eng, fn, reads=(), writes=(), signal=True, chan=None):
        deps = set()
        for k in reads:
            w = self.last_w.get(k)
            if w is not None:
                deps.add(w)
        for k in writes:
            w = self.last_w.get(k)
            if w is not None:
                deps.add(w)
            deps |= self.readers.get(k, set())
        idx = len(self.ops[eng])
        if chan is not None and chan not in self.chans:
            self.chans.append(chan)
        self.ops[eng].append(dict(fn=_freeze(fn), deps=deps, signal=signal, chan=chan))
        ev = (eng, idx)
        for k in reads:
            self.readers.setdefault(k, set()).add(ev)
        for k in writes:
            self.last_w[k] = ev
            self.readers[k] = set()
        return ev

    _uid = [0]

    def emit(self, nc, stack):
        sems = {}
        Prog._uid[0] += 1
        u = "_%d" % Prog._uid[0]
        stack.enter_context(nc.cleanup_on_exit())
        for e in ENGS:
            sems[e] = nc.alloc_semaphore(name="s_" + e + u)
        for c in self.chans:
            sems[("ch", c)] = nc.alloc_semaphore(name="c_" + str(c) + u)
        for e in ENGS:
            cnt = 0
            for o in self.ops[e]:
                if o["chan"] is None and o["signal"]:
                    cnt += 1
                    o["val"] = cnt
            nxt = None
            for o in reversed(self.ops[e]):
                if o["chan"] is not None:
                    continue
                if o["signal"]:
                    nxt = o["val"]
                else:
                    o["val"] = nxt
        ccnt = {}
        for e in ENGS:
            for o in self.ops[e]:
                if o["chan"] is not None:
                    ccnt[o["chan"]] = ccnt.get(o["chan"], 0) + 16
                    o["val"] = ccnt[o["chan"]]

        def resolve(ev):
            e, i = ev
            o = self.ops[e][i]
            if o["chan"] is not None:
                return sems[("ch", o["chan"])], o["val"]
            assert o["val"] is not None, ("dependency on unsignaled op", ev)
            return sems[e], o["val"]

        def emit_eng(ename, eng):
            waited = {}
            for o in self.ops[ename]:
                for d in sorted(o["deps"]):
                    if d[0] == "pe" and ename == "pe":
                        continue
                    s, v = resolve(d)
                    key = id(s)
                    if waited.get(key, 0) < v:
                        eng.wait_ge(s, v)
                        waited[key] = v
                if o["fn"] is None:
                    continue
                ins = o["fn"](eng)
                if o["chan"] is not None:
                    ins.then_inc(sems[("ch", o["chan"])], 16)
                elif o["signal"]:
                    ins.then_inc(sems[ename], 1)
            if ename == "sp":
                for c, v in ccnt.items():
                    eng.wait_ge(sems[("ch", c)], v)

        with nc.Block(no_gpsimd_drain=True) as blk:
            @blk.tensor
            def _(e):
                emit_eng("pe", e)

            @blk.scalar
            def _(e):
                emit_eng("act", e)

            @blk.vector
            def _(e):
                emit_eng("dve", e)

            @blk.gpsimd
            def _(e):
                emit_eng("pool", e)

            @blk.sync
            def _(e):
                emit_eng("sp", e)


def make_consts():
    c = np.zeros((128, 5 * 128 + 16), np.float32)
    mt = np.zeros((128, 2048), np.float32)
    c[:, 0:128] = np.eye(128, dtype=np.float32)
    c[:, 128:256] = np.triu(np.ones((128, 128), np.float32))
    c[63, 256:384] = 1.0
    c[:, 384:512] = 1.0
    c[31, 512:576] = 1.0
    i = np.arange(128)[:, None]
    j = np.arange(128)[None, :]
    for sl in range(16):
        db = 15 - sl
        d = db * 128 + j - i
        m = ((d >= 0) & (d <= 128)).astype(np.float32)
        m += ((d >= 0) & (d % 4 == 0) & (d <= 512)).astype(np.float32)
        m += ((d >= 0) & (d % 16 == 0) & (d <= 2048)).astype(np.float32)
        mt[:, sl * 128:(sl + 1) * 128] = m
    fr = 500000.0 ** (-np.arange(0, 16, 2, dtype=np.float32) / 16)
    c[:, 640:648] = fr.astype(np.float32)[None, :]
    return c, mt


NCONST = 5 * 128 + 16


class Builder:
    def __init__(self, T, depth, stages=("ffn1", "mix", "ffn2")):
        self.T = T
        self.TT = T // 128
        self.depth = depth
        self.stages = stages
        self.TB = min(512, T)
        self.NTB = T // self.TB
        self.NCH = T // 64
        nc = bass.Bass("TRN2", target_bir_lowering=False)
        self.nc = nc
        dt = lambda name, shape, d=F32, kind="ExternalInput": nc.dram_tensor(name, shape, d, kind=kind).ap()
        L = depth
        self.d = dict(
            x=dt("x", [T, D]), positions=dt("positions", [T], I32),
            ffn1_norm=dt("ffn1_norm", [L, D]), ffn1_w_gate=dt("ffn1_w_gate", [L, D, DFF]),
            ffn1_w_up=dt("ffn1_w_up", [L, D, DFF]), ffn1_w_down=dt("ffn1_w_down", [L, DFF, D]),
            mix_norm=dt("mix_norm", [L, D]), w_in=dt("w_in", [L, D, NIN]),
            fox_forget_bias=dt("fox_forget_bias", [L * 4]), hgrn_lower_bounds=dt("hgrn_lower_bounds", [L * 4, 128]),
            hgrn_out_norm=dt("hgrn_out_norm", [L * 4, 128]), w_out=dt("w_out", [L, D, D]),
            ffn2_norm=dt("ffn2_norm", [L, D]), ffn2_w_gate=dt("ffn2_w_gate", [L, D, DFF]),
            ffn2_w_up=dt("ffn2_w_up", [L, D, DFF]), ffn2_w_down=dt("ffn2_w_down", [L, DFF, D]),
            final_norm=dt("final_norm", [D]), consts=dt("consts", [128, NCONST]), mtab=dt("mtab", [128, 2048]),
        )
        self.out = dt("out", [T, D], F32, "ExternalOutput")

    def scope(self):
        return ExitStack()

    def sb(self, st, name, shape, dtype):
        self.uid = getattr(self, "uid", 0) + 1
        return st.enter_context(self.nc.sbuf_tensor("%s_%d" % (name, self.uid), shape, dtype))

    def build(self):
        nc = self.nc
        T, TT = self.T, self.TT
        with ExitStack() as st:
            self.x = self.sb(st, "xres", [128, TT, D], F32)
            self.hT = self.sb(st, "hT", [128, 8, T], BF16)
            self.cst = self.sb(st, "cst", [128, NCONST], F32)
            self.identb = self.sb(st, "identb", [128, 128], BF16)
            self.trib = self.sb(st, "trib", [128, 128], BF16)
            self.onesb = self.sb(st, "onesb", [128, 128], BF16)
            self.mtb = self.sb(st, "mtb", [128, 16 * 128], BF16)
            self.cosb = self.sb(st, "cosb", [128, TT, 8], F32)
            self.sinb = self.sb(st, "sinb", [128, TT, 8], F32)
            self.lb = self.sb(st, "lb", [128, self.depth * 4], F32)
            self.oml = self.sb(st, "oml", [128, self.depth * 4], F32)
            self.nw = self.sb(st, "nw", [128, self.depth * 4], F32)
            self.fbias = self.sb(st, "fbias", [128, self.depth * 4], F32)
            self.epsb = self.sb(st, "epsb", [128, 1], F32)
            self.ps = [st.enter_context(nc.psum_tensor("ps%d" % i, [128, 512], F32)) for i in range(8)]
            self.identf = self.cst[:, 0:128]
            self.trif = self.cst[:, 128:256]
            self.e63f = self.cst[:, 256:384]
            self.onesf = self.cst[:, 384:512]
            self.phase_init()
            for l in range(self.depth):
                if "ffn1" in self.stages:
                    self.phase_ffn(l, "ffn1")
                if "mix" in self.stages:
                    with ExitStack() as ms:
                        self.oT = self.sb(ms, "oT", [128, 8, T], BF16)
                        self.phase_attn(l, "fox")
                        self.phase_attn(l, "dil")
                        self.phase_hgrn(l)
                        if "ffn2" in self.stages:
                            self.phase_ffn(l, "ffn2", with_wout=True)
                        else:
                            self.phase_wout(l)
                elif "ffn2" in self.stages:
                    self.phase_ffn(l, "ffn2")
            self.phase_norm(self.d["final_norm"], "nf", final=True)
        return nc

    def phase_init(self):
        nc, T, TT = self.nc, self.T, self.TT
        L4 = self.depth * 4
        with ExitStack() as st:
            P = Prog()
            posi = self.sb(st, "posi", [128, TT], I32)
            posf = self.sb(st, "posf", [128, TT], F32)
            ang = self.sb(st, "ang", [128, TT, 8], F32)
            tmp = self.sb(st, "tmpi", [128, TT, 8], F32)
            raw = self.sb(st, "rawlb", [L4, 256], F32)
            tl = self.sb(st, "tl", [128, 2 * L4], F32)
            mx = self.sb(st, "mx", [128, 4], F32)
            sm = self.sb(st, "sm", [128, 4], F32)
            x, cst = self.x, self.cst
            P.op("sp", lambda e: e.dma_start(out=cst[:], in_=self.d["consts"]), writes=["cst"], chan="cst")
            nx = 4 if TT >= 4 else TT
            per = TT // nx
            xin = self.d["x"].rearrange("(tt p) d -> p tt d", p=128)
            for i in range(nx):
                P.op("sp", lambda e, i=i: e.dma_start(out=x[:, i * per:(i + 1) * per, :], in_=xin[:, i * per:(i + 1) * per, :]),
                     writes=[("x", t) for t in range(i * per, (i + 1) * per)], chan="x%d" % i)
            P.op("sp", lambda e: e.dma_start(out=posi[:], in_=self.d["positions"].rearrange("(tt p) -> p tt", p=128),
                                             allow_slow_non_contiguous=True), writes=["posi"], chan="pos")
            P.op("sp", lambda e: e.dma_start(out=raw[:, 0:128], in_=self.d["hgrn_lower_bounds"]), writes=["raw0"], chan="r0")
            P.op("sp", lambda e: e.dma_start(out=raw[:, 128:256], in_=self.d["hgrn_out_norm"]), writes=["raw1"], chan="r1")
            P.op("sp", lambda e: e.dma_start(out=self.fbias[:], in_=self.d["fox_forget_bias"].partition_broadcast(128)),
                 writes=["fbias"], chan="fb")
            P.op("dve", lambda e: e.memset(self.epsb[:], EPS), writes=["epsb"])
            P.op("dve", lambda e: e.tensor_copy(out=self.identb[:], in_=cst[:, 0:128]), reads=["cst"], writes=["identb"])
            P.op("dve", lambda e: e.tensor_copy(out=self.trib[:], in_=cst[:, 128:256]), reads=["cst"], writes=["trib"])
            P.op("dve", lambda e: e.tensor_copy(out=self.onesb[:], in_=cst[:, 384:512]), reads=["cst"], writes=["onesb"])
            P.op("pool", lambda e: e.dma_start(out=self.mtb[:], in_=self.d["mtab"]), writes=["mtb"], chan="mtb")
            P.op("dve", lambda e: e.tensor_copy(out=posf[:], in_=posi[:]), reads=["posi"], writes=["posf"])
            fr = cst[:, 640:648]
            P.op("dve", lambda e: e.tensor_tensor(out=ang[:], in0=posf[:].unsqueeze(2).broadcast_to([128, TT, 8]),
                                                  in1=fr.unsqueeze(1).broadcast_to([128, TT, 8]), op=ALU.mult),
                 reads=["posf", "cst"], writes=["ang"])
            twopi = 2.0 * math.pi
            C1 = 6.28125
            C2 = twopi - C1
            ki = self.sb(st, "ki", [128, TT, 8], I32)
            kf = self.sb(st, "kf", [128, TT, 8], F32)
            a2 = self.sb(st, "a2", [128, TT, 8], F32)
            for nm, dst, sh in (("sin", self.sinb, 0.0), ("cos", self.cosb, 0.5 * math.pi)):
                P.op("dve", lambda e: e.tensor_scalar(out=a2[:], in0=ang[:], scalar1=sh, scalar2=None, op0=ALU.add), reads=["ang"], writes=["a2"])
                P.op("dve", lambda e: e.tensor_scalar(out=tmp[:], in0=a2[:], scalar1=1.0 / twopi, scalar2=None, op0=ALU.mult), reads=["a2"], writes=["tmpi"])
                P.op("dve", lambda e: e.tensor_copy(out=ki[:], in_=tmp[:]), reads=["tmpi"], writes=["ki"])
                P.op("dve", lambda e: e.tensor_copy(out=kf[:], in_=ki[:]), reads=["ki"], writes=["kf"])
                P.op("dve", lambda e: e.scalar_tensor_tensor(out=tmp[:], in0=kf[:], scalar=-C1, in1=a2[:], op0=ALU.mult, op1=ALU.add),
                     reads=["kf", "a2"], writes=["tmpi"])
                P.op("dve", lambda e: e.scalar_tensor_tensor(out=tmp[:], in0=kf[:], scalar=-C2, in1=tmp[:], op0=ALU.mult, op1=ALU.add),
                     reads=["kf", "tmpi"], writes=["tmpi"])
                P.op("dve", lambda e: e.tensor_scalar(out=kf[:], in0=tmp[:], scalar1=math.pi, scalar2=twopi, op0=ALU.is_gt, op1=ALU.mult),
                     reads=["tmpi"], writes=["kf"])
                P.op("dve", lambda e: e.tensor_tensor(out=tmp[:], in0=tmp[:], in1=kf[:], op=ALU.subtract), reads=["tmpi", "kf"], writes=["tmpi"])
                P.op("dve", lambda e: e.tensor_scalar(out=kf[:], in0=tmp[:], scalar1=-math.pi, scalar2=twopi, op0=ALU.is_lt, op1=ALU.mult),
                     reads=["tmpi"], writes=["kf"])
                P.op("dve", lambda e: e.tensor_tensor(out=tmp[:], in0=tmp[:], in1=kf[:], op=ALU.add), reads=["tmpi", "kf"], writes=["tmpi"])
                P.op("act", lambda e: e.activation(out=dst[:], in_=tmp[:], func=AF.Sin), reads=["tmpi"], writes=[nm])
            pst = self.ps[0]
            P.op("pe", lambda e: e.matmul(pst[:, 0:L4], lhsT=raw[:, 0:128], rhs=cst[0:L4, 0:L4], start=True, stop=True),
                 reads=["raw0", "cst"], writes=["ps0"])
            P.op("pe", lambda e: e.matmul(pst[:, L4:2 * L4], lhsT=raw[:, 128:256], rhs=cst[0:L4, 0:L4], start=True, stop=True),
                 reads=["raw1", "cst"], writes=["ps0"])
            P.op("dve", lambda e: e.tensor_copy(out=tl[:], in_=pst[:, 0:2 * L4]), reads=["ps0"], writes=["tl"])
            P.op("dve", lambda e: e.tensor_copy(out=self.nw[:], in_=tl[:, L4:2 * L4]), reads=["tl"], writes=["nw"])
            lbv = tl[:, 0:L4].rearrange("p (l h) -> p h l", h=4)
            P.op("dve", lambda e: e.tensor_reduce(out=mx[:], in_=lbv, axis=AX.X, op=ALU.max), reads=["tl"], writes=["mx"])
            P.op("dve", lambda e: e.tensor_tensor(out=lbv, in0=lbv, in1=mx[:].unsqueeze(2).broadcast_to([128, 4, self.depth]), op=ALU.subtract),
                 reads=["tl", "mx"], writes=["tl"])
            P.op("act", lambda e: e.activation(out=tl[:, 0:L4], in_=tl[:, 0:L4], func=AF.Exp), reads=["tl"], writes=["tl"])
            P.op("dve", lambda e: e.tensor_reduce(out=sm[:], in_=lbv, axis=AX.X, op=ALU.add), reads=["tl"], writes=["sm"])
            P.op("dve", lambda e: e.reciprocal(out=sm[:], in_=sm[:]), reads=["sm"], writes=["sm"])
            P.op("dve", lambda e: e.tensor_tensor(out=lbv, in0=lbv, in1=sm[:].unsqueeze(2).broadcast_to([128, 4, self.depth]), op=ALU.mult),
                 reads=["tl", "sm"], writes=["tl"])
            P.op("dve", lambda e: e.memset(self.lb[:, 0:4], 0.0), writes=["lb"])
            for l in range(1, self.depth):
                P.op("dve", lambda e, l=l: e.tensor_tensor(out=self.lb[:, l * 4:(l + 1) * 4], in0=self.lb[:, (l - 1) * 4:l * 4],
                                                           in1=tl[:, l * 4:(l + 1) * 4], op=ALU.add), reads=["lb", "tl"], writes=["lb"])
            P.op("dve", lambda e: e.tensor_scalar(out=self.lb[:], in0=self.lb[:], scalar1=0.0, scalar2=1.0 - 1e-6, op0=ALU.max, op1=ALU.min),
                 reads=["lb"], writes=["lb"])
            P.op("dve", lambda e: e.tensor_scalar(out=self.oml[:], in0=self.lb[:], scalar1=-1.0, scalar2=1.0, op0=ALU.mult, op1=ALU.add),
                 reads=["lb"], writes=["oml"])
            P.emit(nc, st)

    def phase_norm(self, gain, tag, final=False):
        with ExitStack() as st:
            P = Prog()
            self.norm_ops(P, st, gain, final)
            P.emit(self.nc, st)

    def norm_ops(self, P, st, gain, final=False):
        nc, T, TT = self.nc, self.T, self.TT
        x, hT = self.x, self.hT
        if True:
            gb = self.sb(st, "gb", [128, D], F32)
            ss = self.sb(st, "ss", [128, TT], F32)
            rstd = self.sb(st, "rstd", [128, TT], F32)
            junk = self.sb(st, "junk", [128, D], BF16)
            nb = 2
            if final:
                xn = [self.sb(st, "yo%d" % i, [128, D], F32) for i in range(nb)]
            else:
                xn = [self.sb(st, "xn%d" % i, [128, D], BF16) for i in range(nb)]
            P.op("sp", lambda e: e.dma_start(out=gb[:], in_=gain.partition_broadcast(128)), writes=["gb"], chan="gb")
            P.op("dve", lambda e: e.memset(ss[:], 0.0), writes=["ss"])
            outv = self.out.rearrange("(tt p) d -> p tt d", p=128)
            def stats(tt):
                P.op("act", lambda e, tt=tt: e.activation(out=junk[:], in_=x[:, tt, :], func=AF.Square, accum_out=ss[:, tt:tt + 1]),
                     reads=["ss", ("x", tt, 0), ("x", tt, 1)], writes=[("ss", tt), "junk"])
                P.op("act", lambda e, tt=tt: e.activation(out=rstd[:, tt:tt + 1], in_=ss[:, tt:tt + 1], func=AF.Ln, bias=self.epsb[:], scale=1.0 / D),
                     reads=[("ss", tt)], writes=[("rstd", tt)])
                P.op("act", lambda e, tt=tt: e.activation(out=rstd[:, tt:tt + 1], in_=rstd[:, tt:tt + 1], func=AF.Exp, scale=-0.5),
                     reads=[("rstd", tt)], writes=[("rstd", tt)])
            def rest(tt):
                b = tt % nb
                P.op("dve", lambda e, tt=tt, b=b: e.scalar_tensor_tensor(out=xn[b][:], in0=x[:, tt, :], scalar=rstd[:, tt:tt + 1], in1=gb[:],
                                                                          op0=ALU.mult, op1=ALU.mult),
                     reads=[("rstd", tt), "gb", ("x", tt, 0), ("x", tt, 1)], writes=[("xn", b)])
                if final:
                    P.op("sp", lambda e, tt=tt, b=b: e.dma_start(out=outv[:, tt, :], in_=xn[b][:]), reads=[("xn", b)], writes=[("o", tt)],
                         chan="o%d" % b)
                else:
                    pb = self.ps[tt % 2]
                    pT = pb.bitcast(BF16)
                    for kc in range(8):
                        P.op("pe", lambda e, kc=kc, b=b, pT=pT: e.transpose(out=pT[:, kc * 128:(kc + 1) * 128], in_=xn[b][:, kc * 128:(kc + 1) * 128],
                                                                             identity=self.identb[:]),
                             reads=[("xn", b)], writes=[("ps", tt % 2)], signal=(kc == 7))
                    eng = "act" if tt % 2 == 0 else "dve"
                    if eng == "act":
                        P.op("act", lambda e, tt=tt, pT=pT: e.copy(out=hT[:, :, tt * 128:(tt + 1) * 128], in_=pT.rearrange("p (k c) -> p k c", c=128)),
                             reads=[("ps", tt % 2)], writes=[("hT", tt)])
                    else:
                        P.op("dve", lambda e, tt=tt, pT=pT: e.tensor_copy(out=hT[:, :, tt * 128:(tt + 1) * 128], in_=pT.rearrange("p (k c) -> p k c", c=128)),
                             reads=[("ps", tt % 2)], writes=[("hT", tt)])
            for tt in range(TT + 1):
                if tt < TT:
                    stats(tt)
                if tt >= 1:
                    rest(tt - 1)
            if final:
                P.op("sp", None, reads=[("o", tt) for tt in range(TT)])

    def phase_ffn(self, l, which, with_wout=False):
        nc, T, TT, TB, NTB = self.nc, self.T, self.TT, self.TB, self.NTB
        x, hT, ps = self.x, self.hT, self.ps
        wg_d = self.d[which + "_w_gate"][l].rearrange("(kc p) n -> p kc n", p=128)
        wu_d = self.d[which + "_w_up"][l].rearrange("(kc p) n -> p kc n", p=128)
        wd_d = self.d[which + "_w_down"][l].rearrange("(j p) n -> p j n", p=128)
        G = 2
        NG = DFF // (128 * G)
        with ExitStack() as st:
            P = Prog()
            wg = [self.sb(st, "wg%d" % i, [128, 8, 128 * G], BF16) for i in range(2)]
            wu = [self.sb(st, "wu%d" % i, [128, 8, 128 * G], BF16) for i in range(2)]
            wd = [self.sb(st, "wd%d" % i, [128, G, D], BF16) for i in range(2)]
            aT = [self.sb(st, "aT%d" % i, [128, G, T], BF16) for i in range(2)]
            sg = [self.sb(st, "sg%d" % i, [128, TB], BF16) for i in range(2)]
            cnt = {"gu": 0, "d": 0}

            def load(g):
                s = g % 2
                c0 = g * 128 * G
                P.op("pool", lambda e: e.dma_start(out=wg[s][:], in_=wg_d[:, :, c0:c0 + 128 * G]), writes=[("wg", s)], chan="wg%d" % s)
                P.op("pool", lambda e: e.dma_start(out=wu[s][:], in_=wu_d[:, :, c0:c0 + 128 * G]), writes=[("wu", s)], chan="wu%d" % s)
                P.op("pool", lambda e: e.dma_start(out=wd[s][:], in_=wd_d[:, g * G:(g + 1) * G, :]), writes=[("wd", s)], chan="wd%d" % s)

            def GU(g):
                s = g % 2
                for j in range(G):
                    for tb in range(NTB):
                        c = cnt["gu"]
                        cnt["gu"] += 1
                        pg, pu = ps[c % 2], ps[2 + c % 2]
                        tsl = slice(tb * TB, (tb + 1) * TB)
                        htk = [("hT", t) for t in range(tb * TB // 128, (tb + 1) * TB // 128)]
                        for kc in range(8):
                            P.op("pe", lambda e, kc=kc, pg=pg, tsl=tsl: e.matmul(pg[:, 0:TB], lhsT=wg[s][:, kc, j * 128:(j + 1) * 128], rhs=hT[:, kc, tsl],
                                                                                 start=(kc == 0), stop=(kc == 7)),
                                 reads=[("wg", s)] + htk, writes=[("ps", c % 2)], signal=(kc == 7))
                        for kc in range(8):
                            P.op("pe", lambda e, kc=kc, pu=pu, tsl=tsl: e.matmul(pu[:, 0:TB], lhsT=wu[s][:, kc, j * 128:(j + 1) * 128], rhs=hT[:, kc, tsl],
                                                                                 start=(kc == 0), stop=(kc == 7)),
                                 reads=[("wu", s)] + htk, writes=[("ps", 2 + c % 2)], signal=(kc == 7))
                        P.op("act", lambda e, pg=pg, c=c: e.activation(out=sg[c % 2][:], in_=pg[:, 0:TB], func=AF.Silu),
                             reads=[("ps", c % 2)], writes=[("sg", c % 2)])
                        P.op("dve", lambda e, pu=pu, c=c, tsl=tsl: e.tensor_tensor(out=aT[s][:, j, tsl], in0=sg[c % 2][:], in1=pu[:, 0:TB], op=ALU.mult),
                             reads=[("sg", c % 2), ("ps", 2 + c % 2)], writes=[("aT", s, tb)])

            def DN(g):
                s = g % 2
                for tt in range(TT):
                    tb = (tt * 128) // TB
                    for nh in range(2):
                        c = cnt["d"]
                        cnt["d"] += 1
                        pd = ps[4 + c % 4]
                        for j in range(G):
                            P.op("pe", lambda e, j=j, pd=pd, tt=tt, nh=nh: e.matmul(pd[:], lhsT=aT[s][:, j, tt * 128:(tt + 1) * 128],
                                                                                     rhs=wd[s][:, j, nh * 512:(nh + 1) * 512],
                                                                                     start=(j == 0), stop=(j == G - 1)),
                                 reads=[("aT", s, tb), ("wd", s)], writes=[("ps", 4 + c % 4)], signal=(j == G - 1))
                        P.op("dve", lambda e, pd=pd, tt=tt, nh=nh: e.scalar_tensor_tensor(out=x[:, tt, nh * 512:(nh + 1) * 512], in0=pd[:], scalar=0.5,
                                                                                          in1=x[:, tt, nh * 512:(nh + 1) * 512], op0=ALU.mult, op1=ALU.add),
                             reads=[("ps", 4 + c % 4)], writes=[("x", tt, nh)])

            if with_wout:
                self.wout_ops(P, st, l)
            load(0)
            load(1)
            self.norm_ops(P, st, self.d[which + "_norm"][l])
            GU(0)
            for g in range(1, NG):
                GU(g)
                DN(g - 1)
                if g + 1 < NG:
                    load(g + 1)
            DN(NG - 1)
            P.emit(nc, st)

    def phase_attn(self, l, kind):
        nc, T, TT, TB, NTB = self.nc, self.T, self.TT, self.TB, self.NTB
        x, hT, ps, oT = self.x, self.hT, self.ps, self.oT
        win = self.d["w_in"][l].rearrange("(kc p) n -> p kc n", p=128)
        fox = kind == "fox"
        qc0 = 0 if fox else 772
        vc0 = 512 if fox else 1284
        nv = 260 if fox else 256
        obase = 0 if fox else 2
        with ExitStack() as st:
            P = Prog()
            wqk = self.sb(st, "wqk", [128, 8, 512], BF16)
            wv = self.sb(st, "wv", [128, 8, nv], BF16)
            qT = self.sb(st, "qT", [128, 2, T], BF16)
            kT = self.sb(st, "kT", [128, 2, T], BF16)
            va = self.sb(st, "va", [128, TT, 4, 72], BF16)
            otok = self.sb(st, "otok", [128, TT, 256], BF16)
            NPB = 6
            pt = [self.sb(st, "pt%d" % i, [128, 512], BF16) for i in range(NPB)]
            rcp = [self.sb(st, "rcp%d" % i, [128, 1], F32) for i in range(2)]
            P.op("pool", lambda e: e.dma_start(out=wqk[:], in_=win[:, :, qc0:qc0 + 512]), writes=["wqk"], chan="wqk")
            P.op("pool", lambda e: e.dma_start(out=wv[:], in_=win[:, :, vc0:vc0 + nv]), writes=["wv"], chan="wv")
            P.op("pool", lambda e: e.memset(va[:, :, :, 64:65], 1.0), writes=["va1"])
            if fox:
                self.norm_ops(P, st, self.d["mix_norm"][l])
            if fox:
                lfr = self.sb(st, "lfr", [128, TT, 4], F32)
                cc = self.sb(st, "cc", [128, TT, 4], F32)
                carry = self.sb(st, "carry", [128, TT, 4], F32)
                tot = self.sb(st, "tot", [128, TT, 4], F32)
                cmid = self.sb(st, "cmid", [128, TT, 4], F32)
                bias = self.sb(st, "biasf", [128, 4, TT, TT], F32)
                wexp = bias
                vs = [self.sb(st, "vs%d" % i, [128, 4, 72], BF16) for i in range(NPB)]
                P.op("pool", lambda e: e.memset(bias[:], 0.0), writes=["bias"])
                cn = 0
                for wi, dst in ((0, qT), (1, kT)):
                    for ch in range(2):
                        for tb in range(NTB):
                            pb = ps[cn % 2]
                            tsl = slice(tb * TB, (tb + 1) * TB)
                            for kc in range(8):
                                P.op("pe", lambda e, kc=kc, pb=pb, tsl=tsl, wi=wi, ch=ch: e.matmul(
                                    pb[:, 0:TB], lhsT=wqk[:, kc, wi * 256 + ch * 128: wi * 256 + (ch + 1) * 128], rhs=hT[:, kc, tsl],
                                    start=(kc == 0), stop=(kc == 7)), reads=["wqk"] + [("hT", t) for t in range(tb * TB // 128, (tb + 1) * TB // 128)],
                                     writes=[("ps", cn % 2)], signal=(kc == 7))
                            if cn % 2 == 0:
                                P.op("act", lambda e, pb=pb, dst=dst, ch=ch, tsl=tsl: e.copy(out=dst[:, ch, tsl], in_=pb[:, 0:TB]),
                                     reads=[("ps", cn % 2)], writes=[("qk", wi, t) for t in range(tb * TB // 128, (tb + 1) * TB // 128)])
                            else:
                                P.op("dve", lambda e, pb=pb, dst=dst, ch=ch, tsl=tsl: e.tensor_copy(out=dst[:, ch, tsl], in_=pb[:, 0:TB]),
                                     reads=[("ps", cn % 2)], writes=[("qk", wi, t) for t in range(tb * TB // 128, (tb + 1) * TB // 128)])
                            cn += 1
            else:
                qkt = [self.sb(st, "qkt%d" % i, [128, 8, 64], BF16) for i in range(2)]
                rt = [self.sb(st, "rt%d" % i, [128, 8, 8], F32) for i in range(4)]
                for tt in range(TT):
                    pb = ps[tt % 2]
                    b = tt % 2
                    for kc in range(8):
                        P.op("pe", lambda e, kc=kc, pb=pb, tt=tt: e.matmul(pb[:], lhsT=hT[:, kc, tt * 128:(tt + 1) * 128], rhs=wqk[:, kc, :],
                                                                           start=(kc == 0), stop=(kc == 7)),
                             reads=["wqk"], writes=[("ps", tt % 2)], signal=(kc == 7))
                    pv = pb[:].rearrange("p (h d) -> p h d", d=64)
                    cb = self.cosb[:, tt:tt + 1, :].broadcast_to([128, 8, 8])
                    sbb = self.sinb[:, tt:tt + 1, :].broadcast_to([128, 8, 8])
                    rk = [("rt", i) for i in range(4)]
                    P.op("dve", lambda e, pv=pv, cb=cb: e.tensor_tensor(out=rt[0][:], in0=pv[:, :, 0:8], in1=cb, op=ALU.mult), reads=[("ps", tt % 2)], writes=[rk[0]])
                    P.op("dve", lambda e, pv=pv, sbb=sbb: e.tensor_tensor(out=rt[1][:], in0=pv[:, :, 8:16], in1=sbb, op=ALU.mult), reads=[("ps", tt % 2)], writes=[rk[1]])
                    P.op("dve", lambda e, pv=pv, cb=cb: e.tensor_tensor(out=rt[2][:], in0=pv[:, :, 8:16], in1=cb, op=ALU.mult), reads=[("ps", tt % 2)], writes=[rk[2]])
                    P.op("dve", lambda e, pv=pv, sbb=sbb: e.tensor_tensor(out=rt[3][:], in0=pv[:, :, 0:8], in1=sbb, op=ALU.mult), reads=[("ps", tt % 2)], writes=[rk[3]])
                    P.op("dve", lambda e, b=b: e.tensor_tensor(out=qkt[b][:, :, 0:8], in0=rt[0][:], in1=rt[1][:], op=ALU.subtract),
                         reads=[rk[0], rk[1]], writes=[("qkt", b)])
                    P.op("dve", lambda e, b=b: e.tensor_tensor(out=qkt[b][:, :, 8:16], in0=rt[2][:], in1=rt[3][:], op=ALU.add),
                         reads=[rk[2], rk[3]], writes=[("qkt", b)])
                    P.op("dve", lambda e, b=b, pv=pv: e.tensor_copy(out=qkt[b][:, :, 16:64], in_=pv[:, :, 16:64]), reads=[("ps", tt % 2)], writes=[("qkt", b)])
                    pT = ps[2 + tt % 2].bitcast(BF16)
                    qf = qkt[b][:].rearrange("p h d -> p (h d)")
                    for c4 in range(4):
                        P.op("pe", lambda e, c4=c4, pT=pT, qf=qf: e.transpose(out=pT[:, c4 * 128:(c4 + 1) * 128], in_=qf[:, c4 * 128:(c4 + 1) * 128],
                                                                               identity=self.identb[:]),
                             reads=[("qkt", b)], writes=[("ps", 2 + tt % 2)], signal=(c4 == 3))
                    P.op("act", lambda e, pT=pT, tt=tt: e.copy(out=qT[:, :, tt * 128:(tt + 1) * 128], in_=pT[:, 0:256].rearrange("p (k c) -> p k c", c=128)),
                         reads=[("ps", 2 + tt % 2)], writes=[("qk", 0, tt)])
                    P.op("act", lambda e, pT=pT, tt=tt: e.copy(out=kT[:, :, tt * 128:(tt + 1) * 128], in_=pT[:, 256:512].rearrange("p (k c) -> p k c", c=128)),
                         reads=[("ps", 2 + tt % 2)], writes=[("qk", 1, tt)])
            for tt in range(TT):
                pb = ps[4 + tt % 2]
                for kc in range(8):
                    P.op("pe", lambda e, kc=kc, pb=pb, tt=tt: e.matmul(pb[:, 0:nv], lhsT=hT[:, kc, tt * 128:(tt + 1) * 128], rhs=wv[:, kc, :],
                                                                       start=(kc == 0), stop=(kc == 7)),
                         reads=["wv", ("hT", tt)], writes=[("ps", 4 + tt % 2)], signal=(kc == 7))
                P.op("act", lambda e, pb=pb, tt=tt: e.copy(out=va[:, tt, :, 0:64], in_=pb[:, 0:256].rearrange("p (h d) -> p h d", d=64)),
                     reads=[("ps", 4 + tt % 2)], writes=[("va", tt)])
                if fox:
                    P.op("dve", lambda e, pb=pb, tt=tt: e.tensor_tensor(out=lfr[:, tt, :], in0=pb[:, 256:260], in1=self.fbias[:, l * 4:(l + 1) * 4], op=ALU.add),
                         reads=[("ps", 4 + tt % 2), ("va", tt)], writes=["lfr"])
            if fox:
                lf2 = lfr[:].rearrange("p t h -> p (t h)")
                N4 = TT * 4
                P.op("act", lambda e: e.activation(out=lf2, in_=lf2, func=AF.Sigmoid), reads=["lfr"], writes=["lfr"])
                P.op("act", lambda e: e.activation(out=lf2, in_=lf2, func=AF.Ln), reads=["lfr"], writes=["lfr"])
                P.op("pe", lambda e: e.matmul(ps[6][:, 0:N4], lhsT=self.trif, rhs=lf2, start=True, stop=True), reads=["lfr"], writes=[("ps", 6)])
                P.op("pe", lambda e: e.matmul(ps[7][:, 0:N4], lhsT=self.onesf, rhs=lf2, start=True, stop=True), reads=["lfr"], writes=[("ps", 7)])
                P.op("dve", lambda e: e.tensor_copy(out=tot[:].rearrange("p t h -> p (t h)"), in_=ps[7][:, 0:N4]), reads=[("ps", 7)], writes=["tot"])
                P.op("dve", lambda e: e.memset(carry[:, 0, :], 0.0), writes=["carry"])
                for tt in range(1, TT):
                    P.op("dve", lambda e, tt=tt: e.tensor_tensor(out=carry[:, tt, :], in0=carry[:, tt - 1, :], in1=tot[:, tt - 1, :], op=ALU.add),
                         reads=["carry", "tot"], writes=["carry"])
                P.op("dve", lambda e: e.tensor_tensor(out=cc[:].rearrange("p t h -> p (t h)"), in0=ps[6][:, 0:N4], in1=carry[:].rearrange("p t h -> p (t h)"), op=ALU.add),
                     reads=[("ps", 6), "carry"], writes=["cc"])
                P.op("pe", lambda e: e.matmul(ps[7][:, 0:N4], lhsT=self.e63f, rhs=cc[:].rearrange("p t h -> p (t h)"), start=True, stop=True),
                     reads=["cc"], writes=[("ps", 7)])
                P.op("dve", lambda e: e.tensor_copy(out=cmid[:].rearrange("p t h -> p (t h)"), in_=ps[7][:, 0:N4]), reads=[("ps", 7)], writes=["cmid"])
                cmv = cmid[:].rearrange("p t h -> p h t")
                for kb in range(TT):
                    for h in range(4):
                        P.op("dve", lambda e, kb=kb, h=h: e.tensor_scalar(out=bias[:, h, kb, kb:TT], in0=cmv[:, h, kb:TT], scalar1=cc[:, kb, h:h + 1], scalar2=None,
                                                                          op0=ALU.subtract), reads=["cmid", "cc"], writes=["bias"])
                P.op("act", lambda e: e.activation(out=wexp[:].rearrange("p h a b -> p (h a b)"), in_=bias[:].rearrange("p h a b -> p (h a b)"), func=AF.Exp),
                     reads=["bias"], writes=["bias", "wexp"])
            items = []
            oc = 0
            for h in range(4):
                for qs in range(TT):
                    groups = [list(range(k0, min(k0 + 4, qs + 1))) for k0 in range(0, qs + 1, 4)]
                    for grp in groups:
                        items.append((h, qs, grp, oc, len(items)))
                    oc += 1

            def front(it):
                h, qs, grp, oc, i_ = it
                ch, po = h // 2, 64 * (h % 2)
                pS = ps[i_ % 6]
                skey = ("ps", i_ % 6)
                pb_i = i_ % NPB
                ng = len(grp)
                qsl = slice(qs * 128, (qs + 1) * 128)
                for i, kb in enumerate(grp):
                    P.op("pe", lambda e: e.matmul(pS[:, i * 128:(i + 1) * 128], lhsT=kT[po:po + 64, ch, kb * 128:(kb + 1) * 128],
                                                  rhs=qT[po:po + 64, ch, qsl], start=True, stop=True),
                         reads=[("qk", 1, kb), ("qk", 0, qs)], writes=[skey], signal=(i == ng - 1))
                P.op("act", lambda e: e.activation(out=pt[pb_i][:, 0:ng * 128], in_=pS[:, 0:ng * 128], func=AF.Exp, scale=0.125),
                     reads=[skey], writes=[("pt", pb_i)])
                if fox:
                    k0 = grp[0]
                    P.op("dve", lambda e: e.tensor_tensor(out=vs[pb_i][:, 0:ng, 0:65], in0=va[:, k0:k0 + ng, h, 0:65],
                                                          in1=wexp[:, h, k0:k0 + ng, qs].unsqueeze(2).broadcast_to([128, ng, 65]), op=ALU.mult),
                         reads=["wexp", "va1"] + [("va", kb) for kb in grp], writes=[("vs", pb_i)])
                    if grp[-1] == qs:
                        dsl = slice((ng - 1) * 128, ng * 128)
                        P.op("dve", lambda e: e.tensor_tensor(out=pt[pb_i][:, dsl], in0=pt[pb_i][:, dsl], in1=self.trib[:], op=ALU.mult),
                             reads=[("pt", pb_i)], writes=[("pt", pb_i)])
                else:
                    sl0 = 15 - qs + grp[0]
                    P.op("dve", lambda e: e.tensor_tensor(out=pt[pb_i][:, 0:ng * 128], in0=pt[pb_i][:, 0:ng * 128],
                                                          in1=self.mtb[:, sl0 * 128:(sl0 + ng) * 128], op=ALU.mult),
                         reads=[("pt", pb_i)], writes=[("pt", pb_i)])

            def back(it):
                h, qs, grp, oc, i_ = it
                pO = ps[6 + oc % 2]
                okey = ("ps", 6 + oc % 2)
                pb_i = i_ % NPB
                for i, kb in enumerate(grp):
                    rhs = vs[pb_i][:, i, 0:65] if fox else va[:, kb, h, 0:65]
                    P.op("pe", lambda e: e.matmul(pO[:, 0:65], lhsT=pt[pb_i][:, i * 128:(i + 1) * 128], rhs=rhs,
                                                  start=(kb == 0), stop=(kb == qs)),
                         reads=[("pt", pb_i), ("va", kb), "va1", ("vs", pb_i)], writes=[okey], signal=(kb == qs))
                if grp[-1] == qs:
                    r = rcp[oc % 2]
                    P.op("dve", lambda e: e.reciprocal(out=r[:], in_=pO[:, 64:65]), reads=[okey], writes=[("rcp", oc % 2)])
                    P.op("act", lambda e: e.activation(out=otok[:, qs, h * 64:(h + 1) * 64], in_=pO[:, 0:64], func=AF.Copy, scale=r[:]),
                         reads=[okey, ("rcp", oc % 2)], writes=[("otok", qs)])

            LA = 5
            for i in range(len(items) + LA):
                if i < len(items):
                    front(items[i])
                if i - LA >= 0:
                    back(items[i - LA])
            for tt in range(TT):
                pT = ps[4 + tt % 2].bitcast(BF16)
                for c2 in range(2):
                    P.op("pe", lambda e, c2=c2, pT=pT, tt=tt: e.transpose(out=pT[:, c2 * 128:(c2 + 1) * 128], in_=otok[:, tt, c2 * 128:(c2 + 1) * 128],
                                                                           identity=self.identb[:]),
                         reads=[("otok", tt)], writes=[("ps", 4 + tt % 2)], signal=(c2 == 1))
                P.op("dve", lambda e, pT=pT, tt=tt: e.tensor_copy(out=oT[:, obase:obase + 2, tt * 128:(tt + 1) * 128],
                                                                    in_=pT[:, 0:256].rearrange("p (k c) -> p k c", c=128)),
                     reads=[("ps", 4 + tt % 2)], writes=[("oT", tt)])
            P.emit(nc, st)

    def phase_hgrn(self, l):
        nc, T, TT, TB, NTB, NCH = self.nc, self.T, self.TT, self.TB, self.NTB, self.NCH
        x, hT, ps, oT = self.x, self.hT, self.ps, self.oT
        win = self.d["w_in"][l].rearrange("(kc p) n -> p kc n", p=128)
        CPB = TB // 64
        with ExitStack() as st:
            P = Prog()
            w4 = [self.sb(st, "w4_%d" % i, [128, 8, 4, 128], BF16) for i in range(2)]
            A2 = [self.sb(st, "hA%d" % i, [128, TB], F32) for i in range(2)]
            C2 = [self.sb(st, "hC%d" % i, [128, TB], F32) for i in range(2)]
            D2 = [self.sb(st, "hD%d" % i, [128, TB], F32) for i in range(2)]
            Q2 = [self.sb(st, "hqs%d" % i, [128, TB], BF16) for i in range(2)]
            qt2 = [self.sb(st, "hqt%d" % i, [128, TB], BF16) for i in range(2)]
            kt2 = [self.sb(st, "hkt%d" % i, [128, TB], BF16) for i in range(2)]
            vt2 = [self.sb(st, "hv%d" % i, [64, CPB, 128], BF16) for i in range(2)]
            mm2 = [self.sb(st, "hm%d" % i, [128, CPB], F32) for i in range(2)]
            dd2 = [self.sb(st, "hd%d" % i, [128, CPB], F32) for i in range(2)]
            rr2 = [self.sb(st, "hr%d" % i, [128, CPB], F32) for i in range(2)]
            rmask = self.sb(st, "rmask", [128, TB], F32)
            S2 = [self.sb(st, "hS%d" % i, [128, 128], F32) for i in range(2)]
            Mc = [self.sb(st, "hMc%d" % i, [128, 128], BF16) for i in range(2)]
            tmpkv = self.sb(st, "hkv", [128, CPB, 128], F32)
            ktok = [self.sb(st, "hktok%d" % i, [64, CPB, 128], BF16) for i in range(2)]
            atm = [self.sb(st, "hatm%d" % i, [64, CPB, 64], BF16) for i in range(2)]
            sq = self.sb(st, "hsq", [128, TB], BF16)
            rs = self.sb(st, "hrs", [128, TB], F32)
            sgt = self.sb(st, "hsg", [128, TB], BF16)
            P.op("pool", lambda e: e.memset(rmask[:], 1.0), writes=["rmask"])
            P.op("pool", lambda e: e.memset(rmask[:].rearrange("p (c t) -> p c t", t=64)[:, :, 0:1], 0.0), writes=["rmask"])

            def loadw(h):
                s = h % 2
                for i, c0 in enumerate((1540, 2052, 2564, 3076)):
                    P.op("pool", lambda e: e.dma_start(out=w4[s][:, :, i, :], in_=win[:, :, c0 + h * 128: c0 + (h + 1) * 128]),
                         writes=[("w4", s, i)], chan="w4_%d_%d" % (s, i))

            steps = [(h, tb) for h in range(4) for tb in range(NTB)]
            N = len(steps)
            cnt = {"pc": 0, "vc": 0}
            pA = ps[4]
            pK = ps[5].bitcast(BF16)

            def stage1(i):
                h, tb = steps[i]
                s, p2 = h % 2, i % 2
                lbc = self.lb[:, l * 4 + h: l * 4 + h + 1]
                omc = self.oml[:, l * 4 + h: l * 4 + h + 1]
                tsl = slice(tb * TB, (tb + 1) * TB)
                A, C, Dd, qs_ = A2[p2], C2[p2], D2[p2], Q2[p2]
                qt, kt, vtok, mm, dd, rr = qt2[p2], kt2[p2], vt2[p2], mm2[p2], dd2[p2], rr2[p2]
                kA, kC, kD, kQ = ("hA", p2), ("hC", p2), ("hD", p2), ("hQ", p2)
                for (wi, dst, fn, kk_) in ((1, A, AF.Sigmoid, kA), (0, qs_, AF.Silu, kQ)):
                    pb = ps[cnt["pc"] % 2]
                    pk = ("ps", cnt["pc"] % 2)
                    cnt["pc"] += 1
                    for kc in range(8):
                        P.op("pe", lambda e: e.matmul(pb[:, 0:TB], lhsT=w4[s][:, kc, wi, :], rhs=hT[:, kc, tsl], start=(kc == 0), stop=(kc == 7)),
                             reads=[("w4", s, wi)], writes=[pk], signal=(kc == 7))
                    P.op("act", lambda e: e.activation(out=dst[:], in_=pb[:, 0:TB], func=fn), reads=[pk], writes=[kk_])
                vbanks = []
                for half in range((CPB + 3) // 4):
                    n4 = min(4, CPB - half * 4)
                    pb = ps[cnt["pc"] % 2]
                    pk = ("ps", cnt["pc"] % 2)
                    cnt["pc"] += 1
                    vbanks.append((pb, pk, n4))
                    for j in range(n4):
                        c = tb * CPB + half * 4 + j
                        for kc in range(8):
                            P.op("pe", lambda e: e.matmul(pb[0:64, j * 128:(j + 1) * 128], lhsT=hT[:, kc, c * 64:(c + 1) * 64], rhs=w4[s][:, kc, 2, :],
                                                          start=(kc == 0), stop=(kc == 7)),
                                 reads=[("w4", s, 2)], writes=[pk], signal=(kc == 7 and j == n4 - 1))
                P.op("dve", lambda e: e.tensor_scalar(out=A[:], in0=A[:], scalar1=omc, scalar2=lbc, op0=ALU.mult, op1=ALU.add), reads=[kA], writes=[kA])
                P.op("act", lambda e: e.activation(out=C[:], in_=A[:], func=AF.Ln), reads=[kA], writes=[kC])
                P.op("dve", lambda e: e.tensor_scalar(out=A[:], in0=A[:], scalar1=-1.0, scalar2=1.0, op0=ALU.mult, op1=ALU.add), reads=[kA], writes=[kA])
                P.op("dve", lambda e: e.tensor_tensor_scan(out=Dd[:], data0=rmask[:], data1=C[:], initial=0.0, op0=ALU.mult, op1=ALU.add),
                     reads=[kC, "rmask"], writes=[kD])
                d3 = Dd[:].rearrange("p (c t) -> p c t", t=64)
                c3 = C[:].rearrange("p (c t) -> p c t", t=64)
                P.op("dve", lambda e: e.tensor_tensor(out=c3, in0=d3, in1=d3[:, :, 31:32].broadcast_to([128, CPB, 64]), op=ALU.subtract),
                     reads=[kD], writes=[kC])
                P.op("act", lambda e: e.activation(out=mm[:], in_=d3[:, :, 31], func=AF.Exp), reads=[kD], writes=[("mm", p2)])
                P.op("act", lambda e: e.activation(out=dd[:], in_=d3[:, :, 63], func=AF.Exp), reads=[kD], writes=[("dd", p2)])
                P.op("act", lambda e: e.activation(out=rr[:], in_=c3[:, :, 63], func=AF.Exp), reads=[kC], writes=[("rr", p2)])
                P.op("act", lambda e: e.activation(out=Dd[:], in_=C[:], func=AF.Exp), reads=[kC], writes=[kD])
                P.op("act", lambda e: e.activation(out=C[:], in_=C[:], func=AF.Exp, scale=-1.0), reads=[kC], writes=[kC])
                P.op("dve", lambda e: e.tensor_tensor(out=qt[:], in0=qs_[:], in1=Dd[:], op=ALU.mult), reads=[kQ, kD], writes=[("qt", p2)])
                P.op("dve", lambda e: e.tensor_tensor(out=kt[:], in0=A[:], in1=C[:], op=ALU.mult), reads=[kA, kC], writes=[("kt", p2)])
                for half, (pb, pk, n4) in enumerate(vbanks):
                    if half % 2 == 0:
                        P.op("act", lambda e: e.copy(out=vtok[:, half * 4:half * 4 + n4, :], in_=pb[0:64, 0:n4 * 128].rearrange("p (c v) -> p c v", v=128)),
                             reads=[pk], writes=[("v", p2, half)])
                    else:
                        P.op("dve", lambda e: e.tensor_copy(out=vtok[:, half * 4:half * 4 + n4, :], in_=pb[0:64, 0:n4 * 128].rearrange("p (c v) -> p c v", v=128)),
                             reads=[pk], writes=[("v", p2, half)])

            def passA_pe(i):
                p2 = i % 2
                qt, kt, vtok = qt2[p2], kt2[p2], vt2[p2]
                for cc in range(CPB):
                    cs = slice(cc * 64, (cc + 1) * 64)
                    P.op("pe", lambda e: e.matmul(pA[0:64, cc * 64:(cc + 1) * 64], lhsT=kt[:, cs], rhs=qt[:, cs], start=True, stop=True),
                         reads=[("kt", p2), ("qt", p2)], writes=[("ps", 4)], signal=(cc == CPB - 1))
                for cc in range(CPB):
                    cs = slice(cc * 64, (cc + 1) * 64)
                    P.op("pe", lambda e: e.transpose(out=pK[0:64, cc * 128:(cc + 1) * 128], in_=kt[:, cs], identity=self.identb[:]),
                         reads=[("kt", p2)], writes=[("ps", 5)], signal=(cc == CPB - 1))
                P.op("dve", lambda e: e.tensor_tensor(out=atm[p2][:], in0=pA[0:64, 0:CPB * 64].rearrange("p (c t) -> p c t", t=64),
                                                      in1=self.trib[0:64, 0:64].unsqueeze(1).broadcast_to([64, CPB, 64]), op=ALU.mult),
                     reads=[("ps", 4)], writes=[("atm", p2)])
                P.op("act", lambda e: e.copy(out=ktok[p2][:], in_=pK[0:64, 0:CPB * 128].rearrange("p (c t) -> p c t", t=128)),
                     reads=[("ps", 5)], writes=[("ktok", p2)])
                for cc in range(CPB):
                    bk = 6 + cc // 4
                    P.op("pe", lambda e: e.matmul(ps[bk][:, (cc % 4) * 128:(cc % 4 + 1) * 128], lhsT=ktok[p2][:, cc, :], rhs=vtok[:, cc, :], start=True, stop=True),
                         reads=[("ktok", p2), ("v", p2, cc // 4)], writes=[("ps", bk)], signal=(cc % 4 == 3 or cc == CPB - 1))

            def passA_evac(i):
                p2 = i % 2
                rr = rr2[p2]
                for half in range((CPB + 3) // 4):
                    n4 = min(4, CPB - half * 4)
                    P.op("dve", lambda e: e.tensor_tensor(out=tmpkv[:, half * 4:half * 4 + n4, :],
                                                          in0=ps[6 + half][:, 0:n4 * 128].rearrange("p (c v) -> p c v", v=128),
                                                          in1=rr[:, half * 4:half * 4 + n4].unsqueeze(2).broadcast_to([128, n4, 128]), op=ALU.mult),
                         reads=[("ps", 6 + half), ("rr", p2)], writes=[("tmpkv", half)])

            def passB(i):
                h, tb = steps[i]
                p2 = i % 2
                qt, vtok, mm, dd = qt2[p2], vt2[p2], mm2[p2], dd2[p2]
                pO = ps[2 + p2]
                if tb == 0:
                    P.op("dve", lambda e: e.memset(S2[0][:], 0.0), writes=[("S", 0)])
                for cc in range(CPB):
                    cs = slice(cc * 64, (cc + 1) * 64)
                    b2 = cc % 2
                    co = cc * 64
                    cur, nxt = S2[cc % 2], S2[(cc + 1) % 2]
                    P.op("act", lambda e: e.activation(out=Mc[b2][:], in_=cur[:], func=AF.Copy, scale=mm[:, cc:cc + 1]),
                         reads=[("S", cc % 2), ("mm", p2)], writes=[("Mc", b2)])
                    P.op("dve", lambda e: e.scalar_tensor_tensor(out=nxt[:], in0=cur[:], scalar=dd[:, cc:cc + 1], in1=tmpkv[:, cc, :], op0=ALU.mult, op1=ALU.add),
                         reads=[("S", cc % 2), ("dd", p2), ("tmpkv", cc // 4)], writes=[("S", (cc + 1) % 2)])
                    P.op("pe", lambda e: e.matmul(pO[:, co:co + 64], lhsT=Mc[b2][:], rhs=qt[:, cs], start=True, stop=False),
                         reads=[("Mc", b2), ("qt", p2)], writes=[("ps", 2 + p2)], signal=False)
                    P.op("pe", lambda e: e.matmul(pO[:, co:co + 64], lhsT=vtok[:, cc, :], rhs=atm[p2][:, cc, :], start=False, stop=True),
                         reads=[("atm", p2), ("v", p2, cc // 4)], writes=[("ps", 2 + p2)])

            def epilogue(i):
                h, tb = steps[i]
                s, p2 = h % 2, i % 2
                nwc = self.nw[:, l * 4 + h: l * 4 + h + 1]
                pO = ps[2 + p2]
                tsl = slice(tb * TB, (tb + 1) * TB)
                okeys = [("ps", 2 + p2)]
                P.op("act", lambda e: e.activation(out=sq[:], in_=pO[:, 0:TB], func=AF.Square), reads=okeys, writes=["sq"])
                P.op("pe", lambda e: e.matmul(ps[4][:, 0:TB], lhsT=self.onesb[:], rhs=sq[:], start=True, stop=True), reads=["sq"], writes=[("ps", 4)])
                P.op("act", lambda e: e.activation(out=rs[:], in_=ps[4][:, 0:TB], func=AF.Ln, bias=self.epsb[:], scale=1.0 / 128), reads=[("ps", 4)], writes=["rs"])
                P.op("act", lambda e: e.activation(out=rs[:], in_=rs[:], func=AF.Exp, scale=-0.5), reads=["rs"], writes=["rs"])
                pg = ps[cnt["pc"] % 2]
                pk = ("ps", cnt["pc"] % 2)
                cnt["pc"] += 1
                for kc in range(8):
                    P.op("pe", lambda e: e.matmul(pg[:, 0:TB], lhsT=w4[s][:, kc, 3, :], rhs=hT[:, kc, tsl], start=(kc == 0), stop=(kc == 7)),
                         reads=[("w4", s, 3)], writes=[pk], signal=(kc == 7))
                P.op("act", lambda e: e.activation(out=sgt[:], in_=pg[:, 0:TB], func=AF.Sigmoid), reads=[pk], writes=["sgt"])
                P.op("dve", lambda e: e.tensor_tensor(out=rs[:], in0=pO[:, 0:TB], in1=rs[:], op=ALU.mult), reads=okeys + ["rs"], writes=["rs"])
                P.op("dve", lambda e: e.scalar_tensor_tensor(out=oT[:, 4 + h, tsl], in0=rs[:], scalar=nwc, in1=sgt[:], op0=ALU.mult, op1=ALU.mult),
                     reads=["rs", "sgt"], writes=[("oTh", h, tb)])

            loadw(0)
            loadw(1)
            stage1(0)
            passA_pe(0)
            passA_evac(0)
            if N > 1:
                stage1(1)
            for i in range(N):
                h, tb = steps[i]
                passB(i)
                epilogue(i)
                if tb == NTB - 1 and h + 2 < 4:
                    loadw(h + 2)
                if i + 1 < N:
                    passA_pe(i + 1)
                    passA_evac(i + 1)
                if i + 2 < N:
                    stage1(i + 2)
            P.emit(nc, st)

    def phase_wout(self, l):
        with ExitStack() as st:
            P = Prog()
            self.wout_ops(P, st, l)
            P.emit(self.nc, st)

    def wout_ops(self, P, st, l):
        nc, T, TT = self.nc, self.T, self.TT
        x, ps, oT = self.x, self.ps, self.oT
        wo_d = self.d["w_out"][l].rearrange("(kc p) n -> p kc n", p=128)
        if True:
            wo = self.sb(st, "wo", [128, 8, D], BF16)
            for half in range(2):
                P.op("pool", lambda e, half=half: e.dma_start(out=wo[:, half * 4:(half + 1) * 4, :], in_=wo_d[:, half * 4:(half + 1) * 4, :]),
                     writes=[("wo", half)], chan="wo%d" % half)
            c = 0
            for tt in range(TT):
                for nh in range(2):
                    pb = ps[4 + c % 4]
                    pk = ("ps", 4 + c % 4)
                    c += 1
                    for kc in range(8):
                        P.op("pe", lambda e, kc=kc, pb=pb, tt=tt, nh=nh: e.matmul(pb[:], lhsT=oT[:, kc, tt * 128:(tt + 1) * 128], rhs=wo[:, kc, nh * 512:(nh + 1) * 512],
                                                                                  start=(kc == 0), stop=(kc == 7)),
                             reads=[("wo", kc // 4)], writes=[pk], signal=(kc == 7))
                    P.op("dve", lambda e, pb=pb, tt=tt, nh=nh: e.tensor_tensor(out=x[:, tt, nh * 512:(nh + 1) * 512], in0=pb[:], in1=x[:, tt, nh * 512:(nh + 1) * 512], op=ALU.add),
                         reads=[pk], writes=[("x", tt, nh)])


_NC_CACHE = {}


def prep_inputs(inputs, b, T, depth):
    f = lambda a: np.ascontiguousarray(np.asarray(a, dtype=np.float32))
    m = dict(
        x=f(inputs["x"][b, :T]), positions=np.ascontiguousarray(np.asarray(inputs["positions"][b, :T], dtype=np.int32)),
        fox_forget_bias=f(inputs["fox_forget_bias"][:depth]).reshape(depth * 4),
        hgrn_lower_bounds=f(inputs["hgrn_lower_bounds"][:depth]).reshape(depth * 4, 128),
        hgrn_out_norm=f(inputs["hgrn_out_norm"][:depth]).reshape(depth * 4, 128),
        final_norm=f(inputs["final_norm"]), consts=make_consts()[0], mtab=make_consts()[1],
    )
    for k in ("ffn1_norm", "ffn1_w_gate", "ffn1_w_up", "ffn1_w_down", "mix_norm", "w_in", "w_out", "ffn2_norm",
              "ffn2_w_gate", "ffn2_w_up", "ffn2_w_down"):
        m[k] = f(inputs[k][:depth])
    return m


def kernel(**inputs):
    B, T = inputs["x"].shape[0], inputs["x"].shape[1]
    depth = inputs["w_in"].shape[0]
    key = (T, depth)
    if key not in _NC_CACHE:
        _NC_CACHE[key] = Builder(T, depth).build()
    nc = _NC_CACHE[key]
    shared = prep_inputs(inputs, 0, T, depth)
    in_maps = []
    for b in range(B):
        m = dict(shared)
        m["x"] = np.ascontiguousarray(np.asarray(inputs["x"][b], dtype=np.float32))
        m["positions"] = np.ascontiguousarray(np.asarray(inputs["positions"][b], dtype=np.int32))
        in_maps.append(m)
    res = run_bass_kernel_spmd(nc, in_maps, core_ids=list(range(B)))
    return np.stack([np.asarray(r["out"], dtype=np.float32) for r in res.results], axis=0)
```
